# Optimizing a Trainium2 kernel written in Bass

```python
import math
import jax, jax.numpy as jnp
from jax import lax
import numpy as np

D_MODEL = 2048
BATCH = 8
SEQ = 2048
DEPTH = 2

GATE_WIDTH = D_MODEL
SB_WIDTH = D_MODEL // 2
SB_HEAD_DIM = 128
SB_HEADS = SB_WIDTH // SB_HEAD_DIM
SB_BLOCK = 128
POOL_WIDTH = D_MODEL - SB_WIDTH
POOL_WINDOWS = (2, 4, 8, 16)
POOL_GROUPS = len(POOL_WINDOWS)
POOL_GROUP_DIM = POOL_WIDTH // POOL_GROUPS
EVEN_IN = 3 * SB_WIDTH + POOL_WIDTH + GATE_WIDTH
SCONV_WIDTH = D_MODEL // 2
SCONV_K = 3
CONF_WIDTH = D_MODEL - SCONV_WIDTH
CONF_K = 31
ODD_IN = 3 * SCONV_WIDTH + 2 * CONF_WIDTH + GATE_WIDTH
N_EVEN = (DEPTH + 1) // 2
N_ODD = DEPTH // 2
EPS = 1e-6

kernel_name = "hybrid_stickbreak_pool_shortconv_conformer"


def rms_norm(x, g):
    xf = x.astype(jnp.float32)
    y = xf * lax.rsqrt(jnp.mean(xf * xf, axis=-1, keepdims=True) + EPS)
    return (y * g.astype(jnp.float32)).astype(x.dtype)


def layer_norm(x, g, b):
    xf = x.astype(jnp.float32)
    mu = jnp.mean(xf, axis=-1, keepdims=True)
    var = jnp.mean(jnp.square(xf - mu), axis=-1, keepdims=True)
    y = (xf - mu) * lax.rsqrt(var + EPS)
    return (y * g.astype(jnp.float32) + b.astype(jnp.float32)).astype(x.dtype)


def causal_depthwise_conv(x, w):
    k, c = w.shape
    return lax.conv_general_dilated(
        x, w.astype(x.dtype)[:, None, :], window_strides=(1,), padding=[(k - 1, 0)],
        dimension_numbers=("NWC", "WIO", "NWC"), feature_group_count=c)


def stick_breaking_attention(q, k, v):
    b, s_len, h, dh = q.shape
    scale = 1.0 / math.sqrt(dh)
    outs = []
    for qb in range(s_len // SB_BLOCK):
        q0 = qb * SB_BLOCK
        kend = q0 + SB_BLOCK
        z = jnp.einsum("bqhd,bkhd->bhqk", q[:, q0:kend], k[:, :kend]).astype(jnp.float32) * scale
        t_idx = q0 + jnp.arange(SB_BLOCK)[:, None]
        s_idx = jnp.arange(kend)[None, :]
        mask = s_idx < t_idx
        log_beta = jax.nn.log_sigmoid(z)
        log_1m = jnp.where(mask, jax.nn.log_sigmoid(-z), 0.0)
        log_stay = lax.cumsum(log_1m, axis=3, reverse=True) - log_1m
        wts = jnp.where(mask, jnp.exp(log_beta + log_stay), 0.0)
        outs.append(jnp.einsum("bhqk,bkhd->bqhd", wts.astype(v.dtype), v[:, :kend]))
    return jnp.concatenate(outs, axis=1)


def multiscale_pool(u, pool_w, pool_scale):
    b, s_len, _ = u.shape
    ug = u.reshape(b, s_len, POOL_GROUPS, POOL_GROUP_DIM)
    cs = jnp.cumsum(ug.astype(jnp.float32), axis=1)
    pos1 = jnp.arange(1, s_len + 1)
    pooled = []
    for gi, win in enumerate(POOL_WINDOWS):
        c = cs[:, :, gi]
        prev = jnp.pad(c, ((0, 0), (win, 0), (0, 0)))[:, :s_len]
        count = jnp.minimum(win, pos1).astype(jnp.float32)[None, :, None]
        pooled.append(((c - prev) / count).astype(u.dtype) - ug[:, :, gi])
    pooled = jnp.stack(pooled, axis=2)
    y = jnp.einsum("bsgc,gcd->bsgd", pooled, pool_w).reshape(b, s_len, POOL_WIDTH)
    return y * pool_scale


def even_mixer(h, w_in, pool_w, pool_scale, w_out):
    b, s_len, _ = h.shape
    p = h @ w_in
    q, k, v, u, g = jnp.split(p, np.cumsum([SB_WIDTH, SB_WIDTH, SB_WIDTH, POOL_WIDTH]).tolist(), axis=-1)
    hs = (b, s_len, SB_HEADS, SB_HEAD_DIM)
    a = stick_breaking_attention(q.reshape(hs), k.reshape(hs), v.reshape(hs)).reshape(b, s_len, SB_WIDTH)
    po = multiscale_pool(u, pool_w, pool_scale)
    y = jnp.concatenate([a, po], axis=-1) * jax.nn.silu(g)
    return y @ w_out


def odd_mixer(h, w_in, sconv_w, dconv_w, dconv_b, cnorm_g, cnorm_b, w_out):
    p = h @ w_in
    hc, bc, cc, ga, gb, g = jnp.split(
        p, np.cumsum([SCONV_WIDTH, SCONV_WIDTH, SCONV_WIDTH, CONF_WIDTH, CONF_WIDTH]).tolist(), axis=-1)
    c_out = bc * causal_depthwise_conv(cc * hc, sconv_w)
    d = ga * jax.nn.sigmoid(gb)
    d = causal_depthwise_conv(d, dconv_w) + dconv_b
    d = jax.nn.silu(layer_norm(d, cnorm_g, cnorm_b))
    y = jnp.concatenate([c_out, d], axis=-1) * jax.nn.silu(g)
    return y @ w_out


def setup_inputs(seed: int = 0) -> dict:
    key = jax.random.key(seed)
    ks = jax.random.split(key, 20)
    f32 = jnp.float32
    nrm = lambda k, shape, s: jax.random.normal(k, shape, f32) * s
    return {
        "x": jax.random.normal(ks[0], (BATCH, SEQ, D_MODEL), f32),
        "ln_pre_even": 1.0 + nrm(ks[1], (N_EVEN, D_MODEL), 0.05),
        "w_in_even": nrm(ks[2], (N_EVEN, D_MODEL, EVEN_IN), D_MODEL ** -0.5),
        "pool_w": nrm(ks[3], (N_EVEN, POOL_GROUPS, POOL_GROUP_DIM, POOL_GROUP_DIM), POOL_GROUP_DIM ** -0.5),
        "pool_scale": 1.0 + nrm(ks[4], (N_EVEN, POOL_WIDTH), 0.1),
        "w_out_even": nrm(ks[5], (N_EVEN, D_MODEL, D_MODEL), D_MODEL ** -0.5),
        "ln_post_even": 1.0 + nrm(ks[6], (N_EVEN, D_MODEL), 0.05),
        "ln_pre_odd": 1.0 + nrm(ks[7], (N_ODD, D_MODEL), 0.05),
        "w_in_odd": nrm(ks[8], (N_ODD, D_MODEL, ODD_IN), D_MODEL ** -0.5),
        "sconv_w": nrm(ks[9], (N_ODD, SCONV_K, SCONV_WIDTH), SCONV_K ** -0.5),
        "dconv_w": nrm(ks[10], (N_ODD, CONF_K, CONF_WIDTH), CONF_K ** -0.5),
        "dconv_b": nrm(ks[11], (N_ODD, CONF_WIDTH), 0.02),
        "cnorm_g": 1.0 + nrm(ks[12], (N_ODD, CONF_WIDTH), 0.05),
        "cnorm_b": nrm(ks[13], (N_ODD, CONF_WIDTH), 0.02),
        "w_out_odd": nrm(ks[14], (N_ODD, D_MODEL, D_MODEL), D_MODEL ** -0.5),
        "ln_post_odd": 1.0 + nrm(ks[15], (N_ODD, D_MODEL), 0.05),
    }


def reference(x, ln_pre_even, w_in_even, pool_w, pool_scale, w_out_even, ln_post_even,
              ln_pre_odd, w_in_odd, sconv_w, dconv_w, dconv_b, cnorm_g, cnorm_b, w_out_odd, ln_post_odd):
    for layer in range(DEPTH):
        i = layer // 2
        if layer % 2 == 0:
            h = rms_norm(x, ln_pre_even[i])
            o = even_mixer(h, w_in_even[i], pool_w[i], pool_scale[i], w_out_even[i])
            x = x + rms_norm(o, ln_post_even[i])
        else:
            h = rms_norm(x, ln_pre_odd[i])
            o = odd_mixer(h, w_in_odd[i], sconv_w[i], dconv_w[i], dconv_b[i], cnorm_g[i], cnorm_b[i], w_out_odd[i])
            x = x + rms_norm(o, ln_post_odd[i])
    return x
```

```python
import numpy as np
from contextlib import ExitStack
import concourse.bass as bass
import concourse.mybir as mybir
from concourse.bass_utils import run_bass_kernel_spmd

F32 = mybir.dt.float32
BF16 = mybir.dt.bfloat16
F32R = mybir.dt.float32r
AF = mybir.ActivationFunctionType
ALU = mybir.AluOpType
S = 2048
D = 2048
EPS = 1e-6
QSCALE = float(1.0 / np.sqrt(128.0))
POOL_WINDOWS = (2, 4, 8, 16)


class Tok:
    __slots__ = ("sem", "sid", "val")

    def __init__(self, sem, sid, val):
        self.sem, self.sid, self.val = sem, sid, val


class Buf:
    __slots__ = ("w", "r", "sem", "sid", "cnt")

    def __init__(self):
        self.w = None
        self.r = {}
        self.sem = None
        self.sid = None
        self.cnt = 0


class Eng:
    def __init__(self, h, sem, sid):
        self.h, self.sem, self.sid = h, sem, sid
        self.n = 0
        self.seen = {}

    def wait(self, tok):
        if tok is None:
            return
        if self.seen.get(tok.sid, 0) >= tok.val:
            return
        self.seen[tok.sid] = tok.val
        self.h.wait_ge(tok.sem, tok.val)


class K:
    def __init__(self, nc, st):
        self.nc, self.st = nc, st
        self._sid = 0
        self.pe = self._eng("pe", nc.tensor)
        self.act = self._eng("act", nc.scalar)
        self.dve = self._eng("dve", nc.vector)
        self.pool = self._eng("pool", nc.gpsimd)
        self.sp = self._eng("sp", nc.sync)
        self.engs = [self.pe, self.act, self.dve, self.pool, self.sp]
        self.dma_last = {}

    def new_sem(self, name):
        sem = self.st.enter_context(self.nc.semaphore(name))
        self._sid += 1
        return sem, self._sid

    def _eng(self, name, h):
        sem, sid = self.new_sem("s_" + name)
        return Eng(h, sem, sid)

    def _deps(self, eng, reads, writes):
        for b in reads:
            eng.wait(b.w)
        for b in writes:
            eng.wait(b.w)
            for t in b.r.values():
                eng.wait(t)

    def _reg(self, tok, reads, writes):
        for b in reads:
            b.r[tok.sid] = tok
        for b in writes:
            b.w = tok
            b.r = {}

    def op(self, eng, fns, reads=(), writes=()):
        self._deps(eng, reads, writes)
        if not isinstance(fns, (list, tuple)):
            fns = [fns]
        ins = None
        for fn in fns:
            ins = fn()
        eng.n += 1
        ins.then_inc(eng.sem, 1)
        tok = Tok(eng.sem, eng.sid, eng.n)
        self._reg(tok, reads, writes)
        return tok

    def dma(self, eng, out, in_, owner, reads=(), writes=(), **kw):
        self._deps(eng, reads, writes)
        if owner.sem is None:
            owner.sem, owner.sid = self.new_sem("d%d" % (self._sid + 1))
        owner.cnt += 16
        eng.h.dma_start(out=out, in_=in_, **kw).then_inc(owner.sem, 16)
        tok = Tok(owner.sem, owner.sid, owner.cnt)
        self.dma_last[owner.sid] = tok
        self._reg(tok, reads, writes)
        return tok

    def fence(self):
        toks = [Tok(e.sem, e.sid, e.n) for e in self.engs if e.n > 0] + list(self.dma_last.values())
        for e in self.engs:
            for t in toks:
                e.wait(t)


def build(layers=(0, 1)):
    nc = bass.Bass("TRN2", target_bir_lowering=False)

    def din(name, shape, dt=F32):
        return nc.dram_tensor(name, shape, dt, kind="ExternalInput").ap()

    x_in = din("x", [S, D])
    out_d = nc.dram_tensor("out", [S, D], F32, kind="ExternalOutput").ap()
    x1_d = nc.dram_tensor("x1s", [S, D], F32, kind="Internal").ap()
    xs1_d = nc.dram_tensor("xs1s", [S, D], F32, kind="Internal").ap()
    w0g = din("w0g", [4, 16, 128, 512])
    w0h = din("w0h", [8, 16, 128, 384])
    w0u = din("w0u", [4, 16, 128, 256])
    w0o = din("w0o", [4, 16, 128, 512])
    pw_d = din("pw", [128, 4 * 2 * 256])
    w1c = din("w1c", [8, 16, 128, 256])
    w1g = din("w1g", [8, 16, 128, 128])
    w1s = din("w1s", [8, 16, 128, 512])
    w1o = din("w1o", [4, 16, 128, 512])
    gb_d = din("gb", [4, 128, 2048])
    fm_d = din("fm", [128, 336])
    cst_d = din("cst", [128, 704])

    with ExitStack() as st:
        k = K(nc, st)
        pe, act, dve, pool, sp = k.pe, k.act, k.dve, k.pool, k.sp
        T, A, V, G = nc.tensor, nc.scalar, nc.vector, nc.gpsimd

        def sb(name, shape, dt):
            return st.enter_context(nc.sbuf_tensor(name, shape, dt))

        uniq = {"n": 0}

        def un(name):
            uniq["n"] += 1
            return "%s_%d" % (name, uniq["n"])

        R0 = sb("R0", [128, 16, 2048], BF16)
        R1 = sb("R1", [128, 8, 2048], F32)
        wsl = [sb("wsl0", [128, 16, 512], BF16), sb("wsl1", [128, 16, 512], BF16)]
        wslb = [Buf(), Buf()]
        cst = sb("cst_s", [128, 704], F32)
        fm = sb("fm_s", [128, 336], F32)
        cr = sb("cr", [128, 256], F32)
        zb = sb("zb", [128, 128], BF16)
        stat = sb("stat", [128, 64], F32)
        epsc = sb("epsc", [128, 1], F32)
        psA = st.enter_context(nc.psum_tensor("psA", [128, 2048], F32))
        psB = st.enter_context(nc.psum_tensor("psB", [128, 2048], F32))
        banks = [psA[:, i * 512:(i + 1) * 512] for i in range(4)] + [psB[:, i * 512:(i + 1) * 512] for i in range(4)]
        bankb = [Buf() for _ in range(8)]
        cstb, fmb, pwb, crb, zbb = Buf(), Buf(), Buf(), Buf(), Buf()

        ident = cst[:, 0:128]
        maskS = cst[:, 384:512]
        ones32 = cst[:, 512:640]
        invc = cst[:, 640:704]
        negTRI = cr[:, 0:128].bitcast(F32R)
        negONES = cr[:, 128:256].bitcast(F32R)
        psc = fm[:, 0:8]

        def swv(c, kk):
            return fm[:, 8 + c * 3 + kk: 8 + c * 3 + kk + 1]

        def dwv(c, kk):
            return fm[:, 32 + c * 31 + kk: 32 + c * 31 + kk + 1]

        def dbv(c):
            return fm[:, 280 + c: 281 + c]

        def cgv(c):
            return fm[:, 288 + c: 289 + c]

        def cbv(c):
            return fm[:, 296 + c: 297 + c]

        k.dma(sp, cst[:], cst_d, owner=cstb, writes=[cstb])
        k.dma(sp, fm[:], fm_d, owner=fmb, writes=[fmb])
        k.op(dve, lambda: V.tensor_copy(out=cr[:].bitcast(F32R), in_=cst[:, 128:384]), reads=[cstb], writes=[crb])
        k.op(dve, lambda: V.memset(zb[:], 0.0), writes=[zbb])
        epsb = Buf()
        k.op(dve, lambda: V.memset(epsc[:], EPS), writes=[epsb])

        R0b = [[Buf() for _ in range(4)] for _ in range(16)]
        x1b = [Buf() for _ in range(16)]
        xs1b = [Buf() for _ in range(16)]

        def r0_tg(tg):
            return [R0b[i][j] for i in range(tg * 4, tg * 4 + 4) for j in range(4)]

        def r0_tile(i):
            return R0b[i]

        wstate = {"n": 0}

        def wload(src, ncols):
            s = wstate["n"] % 2
            wstate["n"] += 1
            k.dma(pool, wsl[s][:, :, 0:ncols], src.rearrange("dc p c -> p dc c"), owner=wslb[s], writes=[wslb[s]])
            return s

        pbank = {"n": 0}

        def next_bank(lo=0, hi=4):
            b = lo + pbank["n"] % (hi - lo)
            pbank["n"] += 1
            return b

        def proj_fm(s, c0, tg, b):
            fns = []
            for dc in range(16):
                fns.append(lambda dc=dc: T.matmul(banks[b], lhsT=wsl[s][:, dc, c0:c0 + 128],
                                                  rhs=R0[:, dc, tg * 512:(tg + 1) * 512],
                                                  start=(dc == 0), stop=(dc == 15)))
            k.op(pe, fns, reads=[wslb[s]] + r0_tg(tg), writes=[bankb[b]])

        def phase_norm_in(xin, xinb, gidx, hook=None, pre=None):
            gcol = 304 + (0 if gidx == 0 else 16)
            with ExitStack() as a:
                NB = 4
                xb = [a.enter_context(nc.sbuf_tensor(un("xb%d" % i), [128, 2048], F32)) for i in range(NB)]
                xbb = [Buf() for _ in range(NB)]
                ssb = [Buf() for _ in range(16)]
                junk = a.enter_context(nc.sbuf_tensor(un("junk"), [128, 2048], BF16))
                junkb = Buf()

                def stage_a(i):
                    xt, xtb = xb[i % NB], xbb[i % NB]
                    rd = [xinb[i]] if xinb is not None else []
                    k.dma(sp, xt[:], xin[i * 128:(i + 1) * 128, :], owner=xtb, reads=rd, writes=[xtb])
                    ssc = stat[:, i:i + 1]
                    rsc = stat[:, 16 + i:17 + i]
                    if pre is None:
                        k.op(act, lambda: A.activation(out=junk[:], in_=xt[:], func=AF.Square, accum_out=ssc),
                             reads=[xtb], writes=[junkb, ssb[i]])
                        k.op(act, lambda: A.activation(out=rsc, in_=ssc, func=AF.Sqrt, scale=1.0 / D,
                                                       bias=epsc[:, 0:1]), reads=[ssb[i], epsb], writes=[ssb[i]])
                        k.op(dve, lambda: V.reciprocal(out=rsc, in_=rsc), reads=[ssb[i]], writes=[ssb[i]])
                        sbuf_ = ssb[i]
                    else:
                        return
                    k.op(dve, lambda: V.tensor_scalar(out=xt[:], in0=xt[:], scalar1=rsc, scalar2=None, op0=ALU.mult),
                         reads=[xtb, sbuf_], writes=[xtb])

                def stage_b(i):
                    xt, xtb = xb[i % NB], xbb[i % NB]
                    for jg in range(4):
                        b = next_bank(0, 4)
                        fns = [lambda kk=kk: T.transpose(banks[b][:, kk * 128:(kk + 1) * 128],
                                                         xt[:, (jg * 4 + kk) * 128:(jg * 4 + kk + 1) * 128], ident)
                               for kk in range(4)]
                        k.op(pe, fns, reads=[xtb, cstb], writes=[bankb[b]])
                        for kk in range(4):
                            dc = jg * 4 + kk
                            o = R0[:, dc, i * 128:(i + 1) * 128]
                            src = banks[b][:, kk * 128:(kk + 1) * 128]
                            gsc = fm[:, gcol + dc:gcol + dc + 1]
                            wr = [R0b[i][jg]] if kk == 3 else []
                            if jg < (1 if pre is None else 2):
                                k.op(act, lambda: A.activation(out=o, in_=src, func=AF.Copy, scale=gsc),
                                     reads=[bankb[b], fmb], writes=wr)
                            else:
                                k.op(dve, lambda: V.tensor_scalar(out=o, in0=src, scalar1=gsc, scalar2=None,
                                                                  op0=ALU.mult), reads=[bankb[b], fmb], writes=wr)

                for i in range(NB - 1):
                    stage_a(i)
                for i in range(16):
                    stage_b(i)
                    if i + NB - 1 < 16:
                        stage_a(i + NB - 1)
                    if hook is not None and i % 4 == 3:
                        hook(i // 4)
                k.fence()

        wob = [Buf() for _ in range(4)]

        def prefetch_wo(wo, blk):
            k.dma(pool, R0[:, :, blk * 512:(blk + 1) * 512], wo[blk].rearrange("dc p c -> p dc c"),
                  owner=wob[blk], writes=r0_tg(blk))

        def phase_out(xin, xinb, gidx, wo, ysrc, ybufs, xout, xoutb, nxt=None):
            with ExitStack() as a:
                xr = [a.enter_context(nc.sbuf_tensor(un("xr%d" % i), [128, 2048], F32)) for i in range(2)]
                tmp = a.enter_context(nc.sbuf_tensor(un("tmp"), [128, 2048], F32))
                gbc = a.enter_context(nc.sbuf_tensor(un("gbc2"), [128, 2048], F32))
                xrb = [Buf(), Buf()]
                tmpb, gbcb = Buf(), Buf()
                if nxt is not None:
                    xs = a.enter_context(nc.sbuf_tensor(un("xs"), [128, 2048], F32))
                    xsb = Buf()
                ssb = [Buf() for _ in range(16)]
                k.dma(sp, gbc[:], gb_d[gidx], owner=gbcb, writes=[gbcb])
                for i in range(16):
                    ps = psA if i % 2 == 0 else psB
                    pb = bankb[0:4] if i % 2 == 0 else bankb[4:8]
                    xt, xtb = xr[i % 2], xrb[i % 2]
                    rd = [xinb[i]] if xinb is not None else []
                    k.dma(sp, xt[:], xin[i * 128:(i + 1) * 128, :], owner=xtb, reads=rd, writes=[xtb])
                    for dg in range(4):
                        fns = [lambda fc=fc: T.matmul(ps[:, dg * 512:(dg + 1) * 512],
                                                      lhsT=ysrc(fc, i * 128, (i + 1) * 128),
                                                      rhs=R0[:, fc, dg * 512:(dg + 1) * 512],
                                                      start=(fc == 0), stop=(fc == 15)) for fc in range(16)]
                        k.op(pe, fns, reads=ybufs(i // 4) + r0_tg(dg), writes=[pb[dg]])
                    ssc = stat[:, 32 + i:33 + i]
                    rsc = stat[:, 48 + i:49 + i]
                    k.op(act, lambda: A.activation(out=tmp[:], in_=ps[:, :], func=AF.Square, accum_out=ssc),
                         reads=pb, writes=[tmpb, ssb[i]])
                    k.op(act, lambda: A.activation(out=rsc, in_=ssc, func=AF.Sqrt, scale=1.0 / D, bias=epsc[:, 0:1]),
                         reads=[ssb[i], epsb], writes=[ssb[i]])
                    k.op(dve, lambda: V.reciprocal(out=rsc, in_=rsc), reads=[ssb[i]], writes=[ssb[i]])
                    k.op(dve, lambda: V.scalar_tensor_tensor(out=tmp[:], in0=ps[:, :], scalar=rsc, in1=gbc[:],
                                                             op0=ALU.mult, op1=ALU.mult),
                         reads=pb + [ssb[i], gbcb], writes=[tmpb])
                    k.op(pool, lambda: G.tensor_tensor(out=xt[:], in0=xt[:], in1=tmp[:], op=ALU.add),
                         reads=[xtb, tmpb], writes=[xtb])
                    wr = [xoutb[i]] if xoutb is not None else []
                    k.dma(sp, xout[i * 128:(i + 1) * 128, :], xt[:], owner=xtb, reads=[xtb], writes=wr)
                    if nxt is not None:
                        nsc = stat[:, i:i + 1]
                        nrc = stat[:, 16 + i:17 + i]
                        k.op(act, lambda: A.activation(out=tmp[:], in_=xt[:], func=AF.Square, accum_out=nsc),
                             reads=[xtb], writes=[tmpb, nxt[i]])
                        k.op(act, lambda: A.activation(out=nrc, in_=nsc, func=AF.Sqrt, scale=1.0 / D,
                                                       bias=epsc[:, 0:1]), reads=[nxt[i], epsb], writes=[nxt[i]])
                        k.op(dve, lambda: V.reciprocal(out=nrc, in_=nrc), reads=[nxt[i]], writes=[nxt[i]])
                        k.op(dve, lambda: V.tensor_scalar(out=xs[:], in0=xt[:], scalar1=nrc, scalar2=None,
                                                          op0=ALU.mult), reads=[xtb, nxt[i]], writes=[xsb])
                        k.dma(sp, xs1_d[i * 128:(i + 1) * 128, :], xs[:], owner=xsb, reads=[xsb], writes=[xs1b[i]])
                k.fence()

        def layer0():
            R1bf = [[Buf() for _ in range(4)] for _ in range(16)]

            def y0(c, lo, hi):
                return R1[:, c // 2, :].bitcast(BF16)[:, (c % 2) * 2048 + lo:(c % 2) * 2048 + hi]

            stream = [(w0g[b], 512) for b in range(4)] + [(w0h[h], 384) for h in range(8)] + \
                     [(w0u[g], 256) for g in range(4)]
            sidx = {"i": 0}

            def wnext():
                if sidx["i"] < len(stream):
                    src, nco = stream[sidx["i"]]
                    sidx["i"] += 1
                    return wload(src, nco)
                return None

            slots = []

            def need(j):
                while len(slots) <= j and sidx["i"] < len(stream):
                    slots.append(wnext())

            need(0)
            need(1)

            def gate_unit(b, cc, tg, lo, hi):
                c = b * 4 + cc
                bk = next_bank(lo, hi)
                proj_fm(slots[b], cc * 128, tg, bk)
                k.op(act, lambda: A.activation(out=y0(c, tg * 512, (tg + 1) * 512), in_=banks[bk],
                                               func=AF.Silu), reads=[bankb[bk]], writes=[R1bf[c][tg]])

            phase_norm_in(x_in, None, 0, hook=lambda tg: [gate_unit(0, cc, tg, 4, 8) for cc in range(4)])
            for b in range(1, 4):
                need(b + 1)
                for cc in range(4):
                    for tg in range(4):
                        gate_unit(b, cc, tg, 0, 4)
            with ExitStack() as a:
                qT = [a.enter_context(nc.sbuf_tensor(un("qT"), [128, 2048], BF16)) for _ in range(2)]
                kT = [a.enter_context(nc.sbuf_tensor(un("kT"), [128, 2048], BF16)) for _ in range(2)]
                Vt = [a.enter_context(nc.sbuf_tensor(un("Vt"), [128, 16, 128], BF16)) for _ in range(2)]
                Et = [a.enter_context(nc.sbuf_tensor(un("Et%d" % i), [128, 512], F32)) for i in range(2)]
                Etb = [Buf(), Buf()]
                Lt = [a.enter_context(nc.sbuf_tensor(un("Lt%d" % i), [128, 512], F32)) for i in range(3)]
                At = [a.enter_context(nc.sbuf_tensor(un("At%d" % i), [128, 512], BF16)) for i in range(3)]
                LS = [a.enter_context(nc.sbuf_tensor(un("LS%d" % i), [128, 512], F32)) for i in range(2)]
                qb = [[Buf() for _ in range(4)] for _ in range(2)]
                kb = [[Buf() for _ in range(4)] for _ in range(2)]
                vb = [[Buf() for _ in range(4)] for _ in range(2)]
                Ltb = [Buf() for _ in range(3)]
                Atb = [Buf() for _ in range(3)]
                LSb = [Buf(), Buf()]
                gstep = {"n": 0}
                pjb = {"n": 0}
                PJ = (0, 1, 7)

                def pj_bank():
                    b_ = PJ[pjb["n"] % 3]
                    pjb["n"] += 1
                    return b_

                def pe_raw(fns, reads, writes):
                    k._deps(pe, reads, writes)
                    for fn in fns:
                        fn()

                def proj_units(h):
                    s = slots[4 + h]
                    P_ = h % 2
                    units = []

                    def qk_piece(col0, tg, part, st_):
                        if part == 0:
                            st_["bk"] = pj_bank()
                        bk = st_["bk"]
                        fns = [lambda dc=dc: T.matmul(banks[bk], lhsT=wsl[s][:, dc, col0:col0 + 128],
                                                      rhs=R0[:, dc, tg * 512:(tg + 1) * 512],
                                                      start=(dc == 0), stop=(dc == 15))
                               for dc in range(part * 4, part * 4 + 4)]
                        rds = [wslb[s]] + r0_tg(tg)
                        if part < 3:
                            pe_raw(fns, rds, [bankb[bk]])
                            return
                        k.op(pe, fns, reads=rds, writes=[bankb[bk]])
                        if col0 == 0:
                            k.op(dve, lambda: V.tensor_scalar(out=qT[P_][:, tg * 512:(tg + 1) * 512], in0=banks[bk],
                                                              scalar1=QSCALE, scalar2=None, op0=ALU.mult),
                                 reads=[bankb[bk]], writes=[qb[P_][tg]])
                        else:
                            k.op(dve, lambda: V.tensor_copy(out=kT[P_][:, tg * 512:(tg + 1) * 512], in_=banks[bk]),
                                 reads=[bankb[bk]], writes=[kb[P_][tg]])

                    def v_piece(tbg, kk, st_):
                        if kk == 0:
                            st_["bk"] = pj_bank()
                        bk = st_["bk"]
                        tb = tbg * 4 + kk
                        fns = [lambda dc=dc: T.matmul(banks[bk][:, kk * 128:(kk + 1) * 128],
                                                      lhsT=R0[:, dc, tb * 128:(tb + 1) * 128],
                                                      rhs=wsl[s][:, dc, 256:384], start=(dc == 0), stop=(dc == 15))
                               for dc in range(16)]
                        rds = [wslb[s]] + r0_tg(tbg)
                        if kk < 3:
                            pe_raw(fns, rds, [bankb[bk]])
                            return
                        k.op(pe, fns, reads=rds, writes=[bankb[bk]])
                        k.op(dve, lambda: V.tensor_copy(out=Vt[P_][:, tbg * 4:(tbg + 1) * 4, :],
                                                        in_=banks[bk].rearrange("p (a b) -> p a b", a=4)),
                             reads=[bankb[bk]], writes=[vb[P_][tbg]])

                    for tg in range(4):
                        for col0 in (0, 128):
                            st_ = {}
                            for part in range(4):
                                units.append(lambda col0=col0, tg=tg, part=part, st_=st_: qk_piece(col0, tg, part, st_))
                    for tbg in range(4):
                        st_ = {}
                        for kk in range(4):
                            units.append(lambda tbg=tbg, kk=kk, st_=st_: v_piece(tbg, kk, st_))
                    return units

                need(4)
                need(5)
                for u in proj_units(0):
                    u()
                for h in range(8):
                    need(4 + h + 2)
                    P = h % 2
                    units = proj_units(h + 1) if h + 1 < 8 else []
                    nun = len(units)
                    steps = []
                    for Gq in range(4):
                        for sbk in range(4 * Gq + 3, -1, -1):
                            diag = sbk >= 4 * Gq
                            c0 = (sbk - 4 * Gq) * 128 if diag else 0
                            steps.append(dict(G=Gq, sb=sbk, diag=diag, c0=c0, n=512 - c0,
                                              first=(sbk == 4 * Gq + 3), last=(sbk == 0), id=gstep["n"]))
                            gstep["n"] += 1
                    lsc = {"cur": 0}

                    def s1(sp_):
                        i_ = sp_["id"]
                        n, c0, Gq, sbk = sp_["n"], sp_["c0"], sp_["G"], sp_["sb"]
                        bz = 2 + i_ % 3
                        q0 = Gq * 512 + c0
                        k.op(pe, lambda: T.matmul(banks[bz][:, 0:n], lhsT=kT[P][:, sbk * 128:(sbk + 1) * 128],
                                                  rhs=qT[P][:, q0:q0 + n], start=True, stop=True),
                             reads=[kb[P][sbk // 4], qb[P][Gq]], writes=[bankb[bz]])
                        l, lb = Lt[i_ % 3], Ltb[i_ % 3]
                        e, eb = Et[i_ % 2], Etb[i_ % 2]
                        k.op(act, lambda: A.activation(out=e[:, 0:n], in_=banks[bz][:, 0:n], func=AF.Exp),
                             reads=[bankb[bz]], writes=[eb])
                        k.op(act, lambda: A.activation(out=l[:, 0:n].bitcast(F32R), in_=e[:, 0:n], func=AF.Ln,
                                                       bias=1.0, scale=1.0), reads=[eb], writes=[lb])
                        if sp_["diag"]:
                            k.op(dve, lambda: V.tensor_tensor(out=l[:, 0:128].bitcast(F32R), in0=l[:, 0:128],
                                                              in1=maskS, op=ALU.mult), reads=[lb, cstb], writes=[lb])

                    def s2(sp_):
                        i_ = sp_["id"]
                        n, c0, Gq, sbk = sp_["n"], sp_["c0"], sp_["G"], sp_["sb"]
                        bz = 2 + i_ % 3
                        l, lb = Lt[i_ % 3], Ltb[i_ % 3]
                        if sp_["first"]:
                            for j in range(2):
                                k.op(pool, lambda j=j: G.tensor_tensor(out=LS[j][:].bitcast(F32R), in0=cst[:, 0:512],
                                                                       in1=cst[:, 0:512], op=ALU.subtract),
                                     reads=[cstb], writes=[LSb[j]])
                            lsc["cur"] = 0
                        cur = lsc["cur"]
                        fns = [lambda: T.matmul(banks[bz][:, 0:n], lhsT=negTRI, rhs=l[:, 0:n].bitcast(F32R),
                                                start=False, stop=sp_["first"], skip_group_check=True)]
                        rds = [lb, crb]
                        if not sp_["first"]:
                            fns.append(lambda: T.matmul(banks[bz][:, 0:n], lhsT=negONES,
                                                        rhs=LS[cur][:, c0:512].bitcast(F32R), start=False, stop=True,
                                                        skip_group_check=True))
                            rds.append(LSb[cur])
                        k.op(pe, fns, reads=rds, writes=[bankb[bz]])
                        if not sp_["last"]:
                            k.op(pool, lambda: G.tensor_tensor(out=LS[1 - cur][:, c0:512].bitcast(F32R),
                                                               in0=LS[cur][:, c0:512], in1=l[:, 0:n], op=ALU.add),
                                 reads=[LSb[cur], lb], writes=[LSb[1 - cur]])
                            lsc["cur"] = 1 - cur
                        at, ab = At[i_ % 3], Atb[i_ % 3]
                        k.op(act, lambda: A.activation(out=at[:, 0:n], in_=banks[bz][:, 0:n], func=AF.Exp),
                             reads=[bankb[bz]], writes=[ab])
                        if sp_["diag"]:
                            k.op(dve, lambda: V.tensor_tensor(out=at[:, 0:128], in0=at[:, 0:128], in1=maskS,
                                                              op=ALU.mult), reads=[ab, cstb], writes=[ab])

                    def s3(sp_):
                        i_ = sp_["id"]
                        n, c0, Gq, sbk = sp_["n"], sp_["c0"], sp_["G"], sp_["sb"]
                        bo = 5 + Gq % 2
                        at, ab = At[i_ % 3], Atb[i_ % 3]
                        fns = []
                        if sp_["first"]:
                            fns.append(lambda: T.matmul(banks[bo], lhsT=zb[:], rhs=qT[P][:, Gq * 512:(Gq + 1) * 512],
                                                        start=True, stop=False))
                        fns.append(lambda: T.matmul(banks[bo][:, c0:512], lhsT=Vt[P][:, sbk, :], rhs=at[:, 0:n],
                                                    start=False, stop=sp_["last"]))
                        k.op(pe, fns, reads=[vb[P][sbk // 4], ab, zbb, qb[P][Gq]], writes=[bankb[bo]])
                        if sp_["last"]:
                            yo = y0(h, Gq * 512, (Gq + 1) * 512)
                            k.op(dve, lambda: V.tensor_tensor(out=yo, in0=banks[bo], in1=yo, op=ALU.mult),
                                 reads=[bankb[bo], R1bf[h][Gq]], writes=[R1bf[h][Gq]])

                    ns = len(steps)
                    for t in range(ns + 2):
                        if t < ns:
                            s1(steps[t])
                        if 0 <= t - 1 < ns:
                            s2(steps[t - 1])
                        if 0 <= t - 2 < ns:
                            s3(steps[t - 2])
                        tgt = (t + 1) * nun // (ns + 2)
                        while units and nun - len(units) < tgt:
                            units.pop(0)()
                    while units:
                        units.pop(0)()
                k.fence()
            with ExitStack() as a:
                U = a.enter_context(nc.sbuf_tensor(un("U"), [128, 2064], F32))
                PA = a.enter_context(nc.sbuf_tensor(un("PA"), [128, 2064], F32))
                PB = a.enter_context(nc.sbuf_tensor(un("PB"), [128, 2064], F32))
                Pt = [a.enter_context(nc.sbuf_tensor(un("Pt%d" % i), [128, 2048], BF16)) for i in range(2)]
                tf = a.enter_context(nc.sbuf_tensor(un("tf"), [128, 16], F32))
                pw = a.enter_context(nc.sbuf_tensor(un("pw_s"), [128, 4 * 2 * 256], BF16))
                k.dma(pool, pw[:], pw_d, owner=pwb, writes=[pwb])
                Ub, PAb, PBb, tfb = Buf(), Buf(), Buf(), Buf()
                Ptb = [Buf(), Buf()]
                for t_, b_ in ((U, Ub), (PA, PAb), (PB, PBb)):
                    k.op(dve, lambda t_=t_: V.memset(t_[:, 0:16], 0.0), writes=[b_])

                def pool_proj(gi, cc):
                    s = slots[12 + gi]
                    bks = []
                    for tg in range(4):
                        bk = next_bank(0, 8)
                        proj_fm(s, cc * 128, tg, bk)
                        bks.append(bk)
                        if gi == 3 and cc == 1:
                            prefetch_wo(w0o, tg)
                    return bks

                def pool_evac(bks):
                    for tg in range(4):
                        bk = bks[tg]
                        o = U[:, 16 + tg * 512:16 + (tg + 1) * 512]
                        k.op(act, lambda: A.activation(out=o, in_=banks[bk], func=AF.Copy),
                             reads=[bankb[bk]], writes=[Ub])

                def pool_chain(gi, cc):
                    w = POOL_WINDOWS[gi]
                    m = gi + 1
                    cur, curb = U, Ub
                    for lvl in range(m):
                        sh = 1 << lvl
                        dst, dstb = (PA, PAb) if lvl % 2 == 0 else (PB, PBb)
                        k.op(dve, lambda: V.tensor_tensor(out=dst[:, 16:2064], in0=cur[:, 16:2064],
                                                          in1=cur[:, 16 - sh:2064 - sh], op=ALU.add),
                             reads=[curb], writes=[dstb])
                        cur, curb = dst, dstb
                    k.op(dve, lambda: V.scalar_tensor_tensor(out=Pt[cc][:], in0=cur[:, 16:2064], scalar=1.0 / w,
                                                             in1=U[:, 16:2064], op0=ALU.mult, op1=ALU.subtract),
                         reads=[curb, Ub], writes=[Ptb[cc]])
                    k.op(dve, lambda: V.tensor_tensor(out=tf[:, 0:w - 1], in0=cur[:, 16:16 + w - 1],
                                                      in1=invc[:, gi * 16:gi * 16 + w - 1], op=ALU.mult),
                         reads=[curb, cstb], writes=[tfb])
                    k.op(dve, lambda: V.tensor_tensor(out=Pt[cc][:, 0:w - 1], in0=tf[:, 0:w - 1],
                                                      in1=U[:, 16:16 + w - 1], op=ALU.subtract),
                         reads=[tfb, Ub], writes=[Ptb[cc]])

                def pool_linear(gi):
                    for oc in range(2):
                        ch = 2 * gi + oc
                        for tg in range(4):
                            bk = next_bank(0, 8)
                            fns = [lambda kc=kc: T.matmul(
                                banks[bk], lhsT=pw[:, (gi * 2 + kc) * 256 + oc * 128:(gi * 2 + kc) * 256 + (oc + 1) * 128],
                                rhs=Pt[kc][:, tg * 512:(tg + 1) * 512], start=(kc == 0), stop=(kc == 1))
                                for kc in range(2)]
                            k.op(pe, fns, reads=[pwb, Ptb[0], Ptb[1]], writes=[bankb[bk]])
                            yo = y0(8 + ch, tg * 512, (tg + 1) * 512)
                            k.op(dve, lambda: V.scalar_tensor_tensor(out=yo, in0=banks[bk], scalar=psc[:, ch:ch + 1],
                                                                     in1=yo, op0=ALU.mult, op1=ALU.mult),
                                 reads=[bankb[bk], fmb, R1bf[8 + ch][tg]], writes=[R1bf[8 + ch][tg]])

                pend = None
                for gi in range(4):
                    need(12 + gi + 1)
                    b0 = pool_proj(gi, 0)
                    pool_evac(b0)
                    if pend is not None:
                        pool_linear(pend)
                    pool_chain(gi, 0)
                    b1 = pool_proj(gi, 1)
                    pool_evac(b1)
                    pool_chain(gi, 1)
                    pend = gi
                pool_linear(pend)
                k.fence()
            return (lambda fc, lo, hi: y0(fc, lo, hi)), (lambda tg: [R1bf[c][tg] for c in range(16)])

        def layer1():
            R1fb = [[Buf() for _ in range(4)] for _ in range(8)]

            def ybf(c16, lo, hi):
                return R1[:, c16 // 2, :].bitcast(BF16)[:, (c16 % 2) * 2048 + lo:(c16 % 2) * 2048 + hi]

            def ybuf(c16, tg):
                return R1fb[c16 // 2][(c16 % 2) * 2 + tg // 2]

            stream = [(w1c[c], 256) for c in range(8)] + [(w1g[c], 128) for c in range(8)] + \
                     [(w1s[c], 512) for c in range(8)]
            sidx = {"i": 0}

            def wnext():
                if sidx["i"] < len(stream):
                    src, nco = stream[sidx["i"]]
                    sidx["i"] += 1
                    return wload(src, nco)
                return None

            slots = [wnext()]
            with ExitStack() as a:
                dpad = [a.enter_context(nc.sbuf_tensor(un("dpad%d" % i), [128, 2080], F32)) for i in range(2)]
                sig = [a.enter_context(nc.sbuf_tensor(un("sig%d" % i), [128, 512], F32)) for i in range(4)]
                dpb = [Buf(), Buf()]
                sigb = [Buf() for _ in range(4)]
                NPT = 16
                dpr = a.enter_context(nc.sbuf_tensor(un("dpr"), [128, 2080], F32))
                dg = a.enter_context(nc.sbuf_tensor(un("dg"), [128, NPT * 128], F32))
                dprb, dgb = Buf(), Buf()
                cvb = {"n": 0}
                for j in range(2):
                    k.op(dve, lambda j=j: V.memset(dpad[j][:, 0:32], 0.0), writes=[dpb[j]])
                k.op(dve, lambda: V.tensor_copy(out=dpr[:, 0:32].bitcast(F32R), in_=dpad[0][:, 0:32]),
                     reads=[dpb[0]], writes=[dprb])
                def stage_a_pe(c, cb=None):
                    slots.append(wnext())
                    s = slots[c]
                    for tg in range(4):
                        b1 = 4 + tg % 2
                        proj_fm(s, 128, tg, b1)
                        k.op(act, lambda: A.activation(out=sig[tg][:], in_=banks[b1], func=AF.Sigmoid),
                             reads=[bankb[b1]], writes=[sigb[tg]])
                        proj_fm(s, 0, tg, tg)
                        if cb is not None:
                            cb(tg)

                def stage_a_dve(c):
                    dp, dpbb = dpad[c % 2], dpb[c % 2]
                    for tg in range(4):
                        k.op(dve, lambda: V.tensor_tensor(out=dp[:, 32 + tg * 512:32 + (tg + 1) * 512], in0=banks[tg],
                                                          in1=sig[tg][:], op=ALU.mult),
                             reads=[bankb[tg], sigb[tg]], writes=[dpbb])

                def stage_b_act(c):
                    dp, dpbb = dpad[c % 2], dpb[c % 2]
                    for tg in range(4):
                        k.op(act, lambda: A.activation(out=R1[:, c, tg * 512:(tg + 1) * 512],
                                                       in_=dp[:, 32 + tg * 512:32 + (tg + 1) * 512], func=AF.Identity,
                                                       scale=dwv(c, 30), bias=dbv(c)),
                             reads=[dpbb, fmb], writes=[R1fb[c][tg]])
                    for tg in range(4):
                        sl32 = slice(32 + tg * 512, 32 + (tg + 1) * 512)
                        k.op(act, lambda: A.activation(out=dpr[:, sl32].bitcast(F32R), in_=dp[:, sl32], func=AF.Copy),
                             reads=[dpbb], writes=[dprb])

                def stage_b_diag(c, part):
                    for kk in range(part * NPT // 4, (part + 1) * NPT // 4):
                        k.op(act, lambda kk=kk: A.activation(out=dg[:, kk * 128:(kk + 1) * 128].bitcast(F32R),
                                                             in_=ident, func=AF.Copy, scale=dwv(c, kk)),
                             reads=[cstb, fmb], writes=[dgb])

                def stage_b_dve(c):
                    dp, dpbb = dpad[c % 2], dpb[c % 2]
                    dst = R1[:, c, :]
                    for kk in range(NPT, 30):
                        k.op(dve, lambda kk=kk: V.scalar_tensor_tensor(out=dst, in0=dp[:, 2 + kk:2 + kk + 2048],
                                                                       scalar=dwv(c, kk), in1=dst,
                                                                       op0=ALU.mult, op1=ALU.add),
                             reads=[dpbb, fmb], writes=R1fb[c])

                def stage_b_pe(c):
                    dst = R1[:, c, :]
                    for tg in range(4):
                        sl = slice(tg * 512, (tg + 1) * 512)
                        bk = 6 + cvb["n"] % 2
                        cvb["n"] += 1
                        fns = [lambda kk=kk: T.matmul(banks[bk], lhsT=dg[:, kk * 128:(kk + 1) * 128].bitcast(F32R),
                                                      rhs=dpr[:, 2 + kk + tg * 512:2 + kk + (tg + 1) * 512].bitcast(F32R),
                                                      start=(kk == 0), stop=(kk == NPT - 1)) for kk in range(NPT)]
                        k.op(pe, fns, reads=[dgb, dprb], writes=[bankb[bk]])
                        k.op(dve, lambda: V.tensor_tensor(out=dst[:, sl], in0=banks[bk], in1=dst[:, sl], op=ALU.add),
                             reads=[bankb[bk]] + R1fb[c], writes=R1fb[c])

                stage_a_pe(0)
                stage_a_dve(0)
                for c in range(8):
                    stage_b_act(c)
                    if c + 1 < 8:
                        stage_a_pe(c + 1, cb=lambda tg, c=c: stage_b_diag(c, tg))
                    else:
                        for part in range(4):
                            stage_b_diag(c, part)
                    stage_b_dve(c)
                    stage_b_pe(c)
                    if c + 1 < 8:
                        stage_a_dve(c + 1)
                k.fence()
            with ExitStack() as a:
                mean = a.enter_context(nc.sbuf_tensor(un("mean"), [128, 2048], F32))
                rstd = a.enter_context(nc.sbuf_tensor(un("rstd"), [128, 2048], F32))
                sq = [a.enter_context(nc.sbuf_tensor(un("sq%d" % i), [128, 512], F32)) for i in range(2)]
                tmc = [a.enter_context(nc.sbuf_tensor(un("tmc%d" % i), [128, 512], F32)) for i in range(2)]
                sgt = [a.enter_context(nc.sbuf_tensor(un("sgt%d" % i), [128, 512], F32)) for i in range(2)]
                meanb = [Buf() for _ in range(4)]
                rstdb = [Buf() for _ in range(4)]
                sqb, tmcb, sgtb = [Buf(), Buf()], [Buf(), Buf()], [Buf(), Buf()]
                n_ = 0
                for tg in range(4):
                    fns = [lambda c=c: T.matmul(banks[tg], lhsT=ones32, rhs=R1[:, c, tg * 512:(tg + 1) * 512],
                                                start=(c == 0), stop=(c == 7)) for c in range(8)]
                    k.op(pe, fns, reads=[cstb] + [R1fb[c][tg] for c in range(8)], writes=[bankb[tg]])
                    for c in range(8):
                        q_, qb_ = sq[n_ % 2], sqb[n_ % 2]
                        n_ += 1
                        k.op(act, lambda: A.activation(out=q_[:], in_=R1[:, c, tg * 512:(tg + 1) * 512],
                                                       func=AF.Square), reads=[R1fb[c][tg]], writes=[qb_])
                        k.op(pe, lambda: T.matmul(banks[4 + tg], lhsT=ones32, rhs=q_[:], start=(c == 0), stop=(c == 7)),
                             reads=[cstb, qb_], writes=[bankb[4 + tg]])
                    sl = slice(tg * 512, (tg + 1) * 512)
                    k.op(act, lambda: A.activation(out=mean[:, sl], in_=banks[tg], func=AF.Copy, scale=1.0 / 1024),
                         reads=[bankb[tg]], writes=[meanb[tg]])
                    k.op(dve, lambda: V.tensor_tensor(out=rstd[:, sl], in0=mean[:, sl], in1=mean[:, sl], op=ALU.mult),
                         reads=[meanb[tg]], writes=[rstdb[tg]])
                    k.op(dve, lambda: V.scalar_tensor_tensor(out=rstd[:, sl], in0=banks[4 + tg], scalar=1.0 / 1024,
                                                             in1=rstd[:, sl], op0=ALU.mult, op1=ALU.subtract),
                         reads=[bankb[4 + tg], rstdb[tg]], writes=[rstdb[tg]])
                    k.op(dve, lambda: V.tensor_scalar(out=rstd[:, sl], in0=rstd[:, sl], scalar1=EPS, scalar2=None,
                                                      op0=ALU.add), reads=[rstdb[tg]], writes=[rstdb[tg]])
                    k.op(act, lambda: A.activation(out=rstd[:, sl], in_=rstd[:, sl], func=AF.Sqrt),
                         reads=[rstdb[tg]], writes=[rstdb[tg]])
                    k.op(dve, lambda: V.reciprocal(out=rstd[:, sl], in_=rstd[:, sl]), reads=[rstdb[tg]],
                         writes=[rstdb[tg]])
                n_ = 0
                for c in range(8):
                    slots.append(wnext())
                    s = slots[8 + c]
                    for tg in range(4):
                        sl = slice(tg * 512, (tg + 1) * 512)
                        bk = next_bank(0, 8)
                        proj_fm(s, 0, tg, bk)
                        g_, gb_ = sgt[n_ % 2], sgtb[n_ % 2]
                        t_, tb_ = tmc[n_ % 2], tmcb[n_ % 2]
                        n_ += 1
                        k.op(act, lambda: A.activation(out=g_[:], in_=banks[bk], func=AF.Silu),
                             reads=[bankb[bk]], writes=[gb_])
                        k.op(dve, lambda: V.tensor_tensor(out=t_[:], in0=R1[:, c, sl], in1=mean[:, sl], op=ALU.subtract),
                             reads=[R1fb[c][tg], meanb[tg]], writes=[tb_])
                        k.op(dve, lambda: V.tensor_tensor(out=t_[:], in0=t_[:], in1=rstd[:, sl], op=ALU.mult),
                             reads=[tb_, rstdb[tg]], writes=[tb_])
                        k.op(act, lambda: A.activation(out=t_[:], in_=t_[:], func=AF.Silu, scale=cgv(c), bias=cbv(c)),
                             reads=[tb_, fmb], writes=[tb_])
                        k.op(dve, lambda: V.tensor_tensor(out=ybf(2 * c, tg * 512, (tg + 1) * 512), in0=t_[:],
                                                          in1=g_[:], op=ALU.mult),
                             reads=[tb_, gb_, R1fb[c][tg]], writes=[ybuf(2 * c, tg)])
                k.fence()
            with ExitStack() as a:
                Tt = a.enter_context(nc.sbuf_tensor(un("Tt"), [128, 2052], F32))
                acc = a.enter_context(nc.sbuf_tensor(un("acc"), [128, 2048], F32))
                bcsg = a.enter_context(nc.sbuf_tensor(un("bcsg"), [128, 2048], F32))
                hct = [a.enter_context(nc.sbuf_tensor(un("hct%d" % i), [128, 512], F32)) for i in range(2)]
                sgt = [a.enter_context(nc.sbuf_tensor(un("sgd%d" % i), [128, 512], F32)) for i in range(2)]
                Ttb, accb, bcsgb = Buf(), Buf(), Buf()
                hctb, sgtb = [Buf(), Buf()], [Buf(), Buf()]
                k.op(dve, lambda: V.memset(Tt[:, 0:4], 0.0), writes=[Ttb])
                for c in range(8):
                    slots.append(wnext())
                    s = slots[16 + c]
                    for tg in range(4):
                        sl = slice(tg * 512, (tg + 1) * 512)
                        g_, gb_ = sgt[tg % 2], sgtb[tg % 2]
                        h_, hb_ = hct[tg % 2], hctb[tg % 2]
                        bk = next_bank(0, 8)
                        proj_fm(s, 384, tg, bk)
                        k.op(act, lambda: A.activation(out=g_[:], in_=banks[bk], func=AF.Silu),
                             reads=[bankb[bk]], writes=[gb_])
                        bk2 = next_bank(0, 8)
                        proj_fm(s, 128, tg, bk2)
                        k.op(dve, lambda: V.tensor_tensor(out=bcsg[:, sl], in0=banks[bk2], in1=g_[:], op=ALU.mult),
                             reads=[bankb[bk2], gb_], writes=[bcsgb])
                        bk3 = next_bank(0, 8)
                        proj_fm(s, 0, tg, bk3)
                        k.op(act, lambda: A.activation(out=h_[:], in_=banks[bk3], func=AF.Copy),
                             reads=[bankb[bk3]], writes=[hb_])
                        bk4 = next_bank(0, 8)
                        proj_fm(s, 256, tg, bk4)
                        if c == 7:
                            prefetch_wo(w1o, tg)
                        k.op(dve, lambda: V.tensor_tensor(out=Tt[:, 4 + tg * 512:4 + (tg + 1) * 512], in0=banks[bk4],
                                                          in1=h_[:], op=ALU.mult),
                             reads=[bankb[bk4], hb_], writes=[Ttb])
                    k.op(dve, lambda: V.tensor_scalar(out=acc[:], in0=Tt[:, 4:2052], scalar1=swv(c, 2), scalar2=None,
                                                      op0=ALU.mult), reads=[Ttb, fmb], writes=[accb])
                    k.op(dve, lambda: V.scalar_tensor_tensor(out=acc[:], in0=Tt[:, 3:2051], scalar=swv(c, 1),
                                                             in1=acc[:], op0=ALU.mult, op1=ALU.add),
                         reads=[Ttb, fmb], writes=[accb])
                    k.op(dve, lambda: V.scalar_tensor_tensor(out=acc[:], in0=Tt[:, 2:2050], scalar=swv(c, 0),
                                                             in1=acc[:], op0=ALU.mult, op1=ALU.add),
                         reads=[Ttb, fmb], writes=[accb])
                    for tg in range(4):
                        sl = slice(tg * 512, (tg + 1) * 512)
                        k.op(dve, lambda: V.tensor_tensor(out=ybf(2 * c + 1, tg * 512, (tg + 1) * 512), in0=acc[:, sl],
                                                          in1=bcsg[:, sl], op=ALU.mult),
                             reads=[accb, bcsgb], writes=[ybuf(2 * c + 1, tg)])
                k.fence()

            def ysrc(fc, lo, hi):
                c16 = 2 * fc + 1 if fc < 8 else 2 * (fc - 8)
                return ybf(c16, lo, hi)

            def ybufs(tg):
                return [R1fb[c][q] for c in range(8) for q in range(4)]

            return ysrc, ybufs

        k.fence()
        if layers == (0, 1):
            ysrc, ybufs = layer0()
            nss = [Buf() for _ in range(16)]
            phase_out(x_in, None, 1, w0o, ysrc, ybufs, x1_d, x1b, nxt=nss)
            phase_norm_in(xs1_d, xs1b, 2, pre=nss)
            ysrc, ybufs = layer1()
            phase_out(x1_d, x1b, 3, w1o, ysrc, ybufs, out_d, None)
        elif layers == (0,):
            ysrc, ybufs = layer0()
            phase_out(x_in, None, 1, w0o, ysrc, ybufs, out_d, None)
        else:
            phase_norm_in(x_in, None, 2)
            ysrc, ybufs = layer1()
            phase_out(x_in, None, 3, w1o, ysrc, ybufs, out_d, None)
        k.fence()
    return nc


def _tile(w):
    return np.ascontiguousarray(w.reshape(16, 128, w.shape[1]))


def prep_weights(ln_pre_even, w_in_even, pool_w, pool_scale, w_out_even, ln_post_even,
                 ln_pre_odd, w_in_odd, sconv_w, dconv_w, dconv_b, cnorm_g, cnorm_b, w_out_odd, ln_post_odd):
    f = np.float32
    W0 = np.asarray(w_in_even, f)[0]
    W1 = np.asarray(w_in_odd, f)[0]
    wo0 = np.asarray(w_out_even, f)[0]
    wo1 = np.asarray(w_out_odd, f)[0]
    m = {}
    m["w0g"] = np.stack([_tile(W0[:, 4096 + b * 512:4096 + (b + 1) * 512]) for b in range(4)])
    m["w0h"] = np.stack([_tile(np.concatenate([W0[:, h * 128:(h + 1) * 128], W0[:, 1024 + h * 128:1024 + (h + 1) * 128],
                                               W0[:, 2048 + h * 128:2048 + (h + 1) * 128]], axis=1)) for h in range(8)])
    m["w0u"] = np.stack([_tile(W0[:, 3072 + g * 256:3072 + (g + 1) * 256]) for g in range(4)])
    m["w0o"] = np.stack([_tile(wo0[:, b * 512:(b + 1) * 512]) for b in range(4)])
    pw = np.asarray(pool_w, f)[0]
    m["pw"] = np.ascontiguousarray(pw.reshape(4, 2, 128, 256).transpose(2, 0, 1, 3).reshape(128, 2048))
    m["w1c"] = np.stack([_tile(np.concatenate([W1[:, 3072 + c * 128:3072 + (c + 1) * 128],
                                               W1[:, 4096 + c * 128:4096 + (c + 1) * 128]], axis=1)) for c in range(8)])
    m["w1g"] = np.stack([_tile(W1[:, 5120 + (8 + c) * 128:5120 + (9 + c) * 128]) for c in range(8)])
    m["w1s"] = np.stack([_tile(np.concatenate([W1[:, c * 128:(c + 1) * 128], W1[:, 1024 + c * 128:1024 + (c + 1) * 128],
                                               W1[:, 2048 + c * 128:2048 + (c + 1) * 128],
                                               W1[:, 5120 + c * 128:5120 + (c + 1) * 128]], axis=1)) for c in range(8)])
    m["w1o"] = np.stack([_tile(wo1[:, b * 512:(b + 1) * 512]) for b in range(4)])
    gvs = [ln_pre_even, ln_post_even, ln_pre_odd, ln_post_odd]
    m["gb"] = np.stack([np.ascontiguousarray(np.broadcast_to(np.asarray(g, f)[0][None, :], (128, 2048))) for g in gvs])
    fm = np.zeros((128, 336), f)
    fm[:, 304:320] = np.asarray(ln_pre_even, f)[0].reshape(16, 128).T
    fm[:, 320:336] = np.asarray(ln_pre_odd, f)[0].reshape(16, 128).T
    fm[:, 0:8] = np.asarray(pool_scale, f)[0].reshape(8, 128).T
    fm[:, 8:32] = np.asarray(sconv_w, f)[0].reshape(3, 8, 128).transpose(2, 1, 0).reshape(128, 24)
    fm[:, 32:280] = np.asarray(dconv_w, f)[0].reshape(31, 8, 128).transpose(2, 1, 0).reshape(128, 248)
    fm[:, 280:288] = np.asarray(dconv_b, f)[0].reshape(8, 128).T
    fm[:, 288:296] = np.asarray(cnorm_g, f)[0].reshape(8, 128).T
    fm[:, 296:304] = np.asarray(cnorm_b, f)[0].reshape(8, 128).T
    m["fm"] = fm
    cst = np.zeros((128, 704), f)
    p = np.arange(128)
    cst[:, 0:128] = np.eye(128, dtype=f)
    cst[:, 128:256] = -(p[:, None] >= p[None, :]).astype(f)
    cst[:, 256:384] = -1.0
    cst[:, 384:512] = (p[:, None] < p[None, :]).astype(f)
    cst[:, 512:640] = 1.0
    for gi, w in enumerate(POOL_WINDOWS):
        t = np.arange(16)
        cst[:, 640 + gi * 16:640 + (gi + 1) * 16] = (1.0 / np.minimum(w, t + 1)).astype(f)[None, :]
    m["cst"] = cst
    return m


def kernel(x, ln_pre_even, w_in_even, pool_w, pool_scale, w_out_even, ln_post_even,
           ln_pre_odd, w_in_odd, sconv_w, dconv_w, dconv_b, cnorm_g, cnorm_b, w_out_odd, ln_post_odd):
    x = np.asarray(x, np.float32)
    m = prep_weights(ln_pre_even, w_in_even, pool_w, pool_scale, w_out_even, ln_post_even,
                     ln_pre_odd, w_in_odd, sconv_w, dconv_w, dconv_b, cnorm_g, cnorm_b, w_out_odd, ln_post_odd)
    nc = build((0, 1))
    in_maps = []
    for b in range(8):
        d = dict(m)
        d["x"] = np.ascontiguousarray(x[b])
        in_maps.append(d)
    res = run_bass_kernel_spmd(nc, in_maps, core_ids=list(range(8)))
    return np.stack([np.asarray(r["out"], np.float32) for r in res.results], axis=0)
```

```python
import numpy as np
from contextlib import ExitStack
import concourse.bass as bass
import concourse.mybir as mybir
from concourse.bass_utils import run_bass_kernel_spmd

F32 = mybir.dt.float32
BF16 = mybir.dt.bfloat16
F32R = mybir.dt.float32r
AF = mybir.ActivationFunctionType
ALU = mybir.AluOpType
S = 2048
D = 2048
EPS = 1e-6
QSCALE = float(1.0 / np.sqrt(128.0))
POOL_WINDOWS = (2, 4, 8, 16)


class Tok:
    __slots__ = ("sem", "sid", "val")

    def __init__(self, sem, sid, val):
        self.sem, self.sid, self.val = sem, sid, val


class Buf:
    __slots__ = ("w", "r", "sem", "sid", "cnt")

    def __init__(self):
        self.w = None
        self.r = {}
        self.sem = None
        self.sid = None
        self.cnt = 0


class Eng:
    def __init__(self, h, sem, sid):
        self.h, self.sem, self.sid = h, sem, sid
        self.n = 0
        self.seen = {}

    def wait(self, tok):
        if tok is None:
            return
        if self.seen.get(tok.sid, 0) >= tok.val:
            return
        self.seen[tok.sid] = tok.val
        self.h.wait_ge(tok.sem, tok.val)


class K:
    def __init__(self, nc, st):
        self.nc, self.st = nc, st
        self._sid = 0
        self.pe = self._eng("pe", nc.tensor)
        self.act = self._eng("act", nc.scalar)
        self.dve = self._eng("dve", nc.vector)
        self.pool = self._eng("pool", nc.gpsimd)
        self.sp = self._eng("sp", nc.sync)
        self.engs = [self.pe, self.act, self.dve, self.pool, self.sp]
        self.dma_last = {}

    def new_sem(self, name):
        sem = self.st.enter_context(self.nc.semaphore(name))
        self._sid += 1
        return sem, self._sid

    def _eng(self, name, h):
        sem, sid = self.new_sem("s_" + name)
        return Eng(h, sem, sid)

    def _deps(self, eng, reads, writes):
        for b in reads:
            eng.wait(b.w)
        for b in writes:
            eng.wait(b.w)
            for t in b.r.values():
                eng.wait(t)

    def _reg(self, tok, reads, writes):
        for b in reads:
            b.r[tok.sid] = tok
        for b in writes:
            b.w = tok
            b.r = {}

    def op(self, eng, fns, reads=(), writes=()):
        self._deps(eng, reads, writes)
        if not isinstance(fns, (list, tuple)):
            fns = [fns]
        ins = None
        for fn in fns:
            ins = fn()
        eng.n += 1
        ins.then_inc(eng.sem, 1)
        tok = Tok(eng.sem, eng.sid, eng.n)
        self._reg(tok, reads, writes)
        return tok

    def dma(self, eng, out, in_, owner, reads=(), writes=(), **kw):
        self._deps(eng, reads, writes)
        if owner.sem is None:
            owner.sem, owner.sid = self.new_sem("d%d" % (self._sid + 1))
        owner.cnt += 16
        eng.h.dma_start(out=out, in_=in_, **kw).then_inc(owner.sem, 16)
        tok = Tok(owner.sem, owner.sid, owner.cnt)
        self.dma_last[owner.sid] = tok
        self._reg(tok, reads, writes)
        return tok

    def fence(self):
        toks = [Tok(e.sem, e.sid, e.n) for e in self.engs if e.n > 0] + list(self.dma_last.values())
        for e in self.engs:
            for t in toks:
                e.wait(t)


def build(layers=(0, 1)):
    nc = bass.Bass("TRN2", target_bir_lowering=False)

    def din(name, shape, dt=F32):
        return nc.dram_tensor(name, shape, dt, kind="ExternalInput").ap()

    x_in = din("x", [S, D])
    out_d = nc.dram_tensor("out", [S, D], F32, kind="ExternalOutput").ap()
    x1_d = nc.dram_tensor("x1s", [S, D], F32, kind="Internal").ap()
    w0g = din("w0g", [4, 16, 128, 512])
    w0h = din("w0h", [8, 16, 128, 384])
    w0u = din("w0u", [4, 16, 128, 256])
    w0o = din("w0o", [4, 16, 128, 512])
    pw_d = din("pw", [128, 4 * 2 * 256])
    w1c = din("w1c", [8, 16, 128, 256])
    w1g = din("w1g", [8, 16, 128, 128])
    w1s = din("w1s", [8, 16, 128, 512])
    w1o = din("w1o", [4, 16, 128, 512])
    gb_d = din("gb", [4, 128, 2048])
    fm_d = din("fm", [128, 336])
    cst_d = din("cst", [128, 704])

    with ExitStack() as st:
        k = K(nc, st)
        pe, act, dve, pool, sp = k.pe, k.act, k.dve, k.pool, k.sp
        T, A, V, G = nc.tensor, nc.scalar, nc.vector, nc.gpsimd

        def sb(name, shape, dt):
            return st.enter_context(nc.sbuf_tensor(name, shape, dt))

        uniq = {"n": 0}

        def un(name):
            uniq["n"] += 1
            return "%s_%d" % (name, uniq["n"])

        R0 = sb("R0", [128, 16, 2048], BF16)
        R1 = sb("R1", [128, 8, 2048], F32)
        wsl = [sb("wsl0", [128, 16, 512], BF16), sb("wsl1", [128, 16, 512], BF16)]
        wslb = [Buf(), Buf()]
        cst = sb("cst_s", [128, 704], F32)
        fm = sb("fm_s", [128, 336], F32)
        cr = sb("cr", [128, 256], F32)
        zb = sb("zb", [128, 128], BF16)
        stat = sb("stat", [128, 64], F32)
        epsc = sb("epsc", [128, 1], F32)
        psA = st.enter_context(nc.psum_tensor("psA", [128, 2048], F32))
        psB = st.enter_context(nc.psum_tensor("psB", [128, 2048], F32))
        banks = [psA[:, i * 512:(i + 1) * 512] for i in range(4)] + [psB[:, i * 512:(i + 1) * 512] for i in range(4)]
        bankb = [Buf() for _ in range(8)]
        cstb, fmb, pwb, crb, zbb = Buf(), Buf(), Buf(), Buf(), Buf()

        ident = cst[:, 0:128]
        maskS = cst[:, 384:512]
        ones32 = cst[:, 512:640]
        invc = cst[:, 640:704]
        negTRI = cr[:, 0:128].bitcast(F32R)
        negONES = cr[:, 128:256].bitcast(F32R)
        psc = fm[:, 0:8]

        def swv(c, kk):
            return fm[:, 8 + c * 3 + kk: 8 + c * 3 + kk + 1]

        def dwv(c, kk):
            return fm[:, 32 + c * 31 + kk: 32 + c * 31 + kk + 1]

        def dbv(c):
            return fm[:, 280 + c: 281 + c]

        def cgv(c):
            return fm[:, 288 + c: 289 + c]

        def cbv(c):
            return fm[:, 296 + c: 297 + c]

        k.dma(sp, cst[:], cst_d, owner=cstb, writes=[cstb])
        k.dma(sp, fm[:], fm_d, owner=fmb, writes=[fmb])
        k.op(dve, lambda: V.tensor_copy(out=cr[:].bitcast(F32R), in_=cst[:, 128:384]), reads=[cstb], writes=[crb])
        k.op(dve, lambda: V.memset(zb[:], 0.0), writes=[zbb])
        epsb = Buf()
        k.op(dve, lambda: V.memset(epsc[:], EPS), writes=[epsb])

        R0b = [[Buf() for _ in range(4)] for _ in range(16)]
        x1b = [Buf() for _ in range(16)]

        def r0_tg(tg):
            return [R0b[i][j] for i in range(tg * 4, tg * 4 + 4) for j in range(4)]

        def r0_tile(i):
            return R0b[i]

        wstate = {"n": 0}

        def wload(src, ncols):
            s = wstate["n"] % 2
            wstate["n"] += 1
            k.dma(pool, wsl[s][:, :, 0:ncols], src.rearrange("dc p c -> p dc c"), owner=wslb[s], writes=[wslb[s]])
            return s

        pbank = {"n": 0}

        def next_bank(lo=0, hi=4):
            b = lo + pbank["n"] % (hi - lo)
            pbank["n"] += 1
            return b

        def proj_fm(s, c0, tg, b):
            fns = []
            for dc in range(16):
                fns.append(lambda dc=dc: T.matmul(banks[b], lhsT=wsl[s][:, dc, c0:c0 + 128],
                                                  rhs=R0[:, dc, tg * 512:(tg + 1) * 512],
                                                  start=(dc == 0), stop=(dc == 15)))
            k.op(pe, fns, reads=[wslb[s]] + r0_tg(tg), writes=[bankb[b]])

        def phase_norm_in(xin, xinb, gidx, hook=None, pre=None):
            gcol = 304 + (0 if gidx == 0 else 16)
            with ExitStack() as a:
                NB = 4
                xb = [a.enter_context(nc.sbuf_tensor(un("xb%d" % i), [128, 2048], F32)) for i in range(NB)]
                xbb = [Buf() for _ in range(NB)]
                ssb = [Buf() for _ in range(16)]
                junk = a.enter_context(nc.sbuf_tensor(un("junk"), [128, 2048], BF16))
                junkb = Buf()

                def stage_a(i):
                    xt, xtb = xb[i % NB], xbb[i % NB]
                    rd = [xinb[i]] if xinb is not None else []
                    k.dma(sp, xt[:], xin[i * 128:(i + 1) * 128, :], owner=xtb, reads=rd, writes=[xtb])
                    ssc = stat[:, i:i + 1]
                    rsc = stat[:, 16 + i:17 + i]
                    if pre is None:
                        k.op(act, lambda: A.activation(out=junk[:], in_=xt[:], func=AF.Square, accum_out=ssc),
                             reads=[xtb], writes=[junkb, ssb[i]])
                        k.op(act, lambda: A.activation(out=rsc, in_=ssc, func=AF.Sqrt, scale=1.0 / D,
                                                       bias=epsc[:, 0:1]), reads=[ssb[i], epsb], writes=[ssb[i]])
                        k.op(dve, lambda: V.reciprocal(out=rsc, in_=rsc), reads=[ssb[i]], writes=[ssb[i]])
                        sbuf_ = ssb[i]
                    else:
                        sbuf_ = pre[i]
                    k.op(dve, lambda: V.tensor_scalar(out=xt[:], in0=xt[:], scalar1=rsc, scalar2=None, op0=ALU.mult),
                         reads=[xtb, sbuf_], writes=[xtb])

                def stage_b(i):
                    xt, xtb = xb[i % NB], xbb[i % NB]
                    for jg in range(4):
                        b = next_bank(0, 4)
                        fns = [lambda kk=kk: T.transpose(banks[b][:, kk * 128:(kk + 1) * 128],
                                                         xt[:, (jg * 4 + kk) * 128:(jg * 4 + kk + 1) * 128], ident)
                               for kk in range(4)]
                        k.op(pe, fns, reads=[xtb, cstb], writes=[bankb[b]])
                        for kk in range(4):
                            dc = jg * 4 + kk
                            o = R0[:, dc, i * 128:(i + 1) * 128]
                            src = banks[b][:, kk * 128:(kk + 1) * 128]
                            gsc = fm[:, gcol + dc:gcol + dc + 1]
                            wr = [R0b[i][jg]] if kk == 3 else []
                            if jg < (1 if pre is None else 2):
                                k.op(act, lambda: A.activation(out=o, in_=src, func=AF.Copy, scale=gsc),
                                     reads=[bankb[b], fmb], writes=wr)
                            else:
                                k.op(dve, lambda: V.tensor_scalar(out=o, in0=src, scalar1=gsc, scalar2=None,
                                                                  op0=ALU.mult), reads=[bankb[b], fmb], writes=wr)

                for i in range(NB - 1):
                    stage_a(i)
                for i in range(16):
                    stage_b(i)
                    if i + NB - 1 < 16:
                        stage_a(i + NB - 1)
                    if hook is not None and i % 4 == 3:
                        hook(i // 4)
                k.fence()

        wob = [Buf() for _ in range(4)]

        def prefetch_wo(wo, blk):
            k.dma(pool, R0[:, :, blk * 512:(blk + 1) * 512], wo[blk].rearrange("dc p c -> p dc c"),
                  owner=wob[blk], writes=r0_tg(blk))

        def phase_out(xin, xinb, gidx, wo, ysrc, ybufs, xout, xoutb, nxt=None):
            with ExitStack() as a:
                xr = [a.enter_context(nc.sbuf_tensor(un("xr%d" % i), [128, 2048], F32)) for i in range(2)]
                tmp = a.enter_context(nc.sbuf_tensor(un("tmp"), [128, 2048], F32))
                gbc = a.enter_context(nc.sbuf_tensor(un("gbc2"), [128, 2048], F32))
                xrb = [Buf(), Buf()]
                tmpb, gbcb = Buf(), Buf()
                ssb = [Buf() for _ in range(16)]
                k.dma(sp, gbc[:], gb_d[gidx], owner=gbcb, writes=[gbcb])
                for i in range(16):
                    ps = psA if i % 2 == 0 else psB
                    pb = bankb[0:4] if i % 2 == 0 else bankb[4:8]
                    xt, xtb = xr[i % 2], xrb[i % 2]
                    rd = [xinb[i]] if xinb is not None else []
                    k.dma(sp, xt[:], xin[i * 128:(i + 1) * 128, :], owner=xtb, reads=rd, writes=[xtb])
                    for dg in range(4):
                        fns = [lambda fc=fc: T.matmul(ps[:, dg * 512:(dg + 1) * 512],
                                                      lhsT=ysrc(fc, i * 128, (i + 1) * 128),
                                                      rhs=R0[:, fc, dg * 512:(dg + 1) * 512],
                                                      start=(fc == 0), stop=(fc == 15)) for fc in range(16)]
                        k.op(pe, fns, reads=ybufs(i // 4) + r0_tg(dg), writes=[pb[dg]])
                    ssc = stat[:, 32 + i:33 + i]
                    rsc = stat[:, 48 + i:49 + i]
                    k.op(act, lambda: A.activation(out=tmp[:], in_=ps[:, :], func=AF.Square, accum_out=ssc),
                         reads=pb, writes=[tmpb, ssb[i]])
                    k.op(act, lambda: A.activation(out=rsc, in_=ssc, func=AF.Sqrt, scale=1.0 / D, bias=epsc[:, 0:1]),
                         reads=[ssb[i], epsb], writes=[ssb[i]])
                    k.op(dve, lambda: V.reciprocal(out=rsc, in_=rsc), reads=[ssb[i]], writes=[ssb[i]])
                    k.op(dve, lambda: V.scalar_tensor_tensor(out=tmp[:], in0=ps[:, :], scalar=rsc, in1=gbc[:],
                                                             op0=ALU.mult, op1=ALU.mult),
                         reads=pb + [ssb[i], gbcb], writes=[tmpb])
                    k.op(pool, lambda: G.tensor_tensor(out=xt[:], in0=xt[:], in1=tmp[:], op=ALU.add),
                         reads=[xtb, tmpb], writes=[xtb])
                    wr = [xoutb[i]] if xoutb is not None else []
                    k.dma(sp, xout[i * 128:(i + 1) * 128, :], xt[:], owner=xtb, reads=[xtb], writes=wr)
                    if nxt is not None:
                        nsc = stat[:, i:i + 1]
                        nrc = stat[:, 16 + i:17 + i]
                        k.op(act, lambda: A.activation(out=tmp[:], in_=xt[:], func=AF.Square, accum_out=nsc),
                             reads=[xtb], writes=[tmpb, nxt[i]])
                        k.op(act, lambda: A.activation(out=nrc, in_=nsc, func=AF.Sqrt, scale=1.0 / D,
                                                       bias=epsc[:, 0:1]), reads=[nxt[i], epsb], writes=[nxt[i]])
                        k.op(dve, lambda: V.reciprocal(out=nrc, in_=nrc), reads=[nxt[i]], writes=[nxt[i]])
                k.fence()

        def layer0():
            R1bf = [[Buf() for _ in range(4)] for _ in range(16)]

            def y0(c, lo, hi):
                return R1[:, c // 2, :].bitcast(BF16)[:, (c % 2) * 2048 + lo:(c % 2) * 2048 + hi]

            stream = [(w0g[b], 512) for b in range(4)] + [(w0h[h], 384) for h in range(8)] + \
                     [(w0u[g], 256) for g in range(4)]
            sidx = {"i": 0}

            def wnext():
                if sidx["i"] < len(stream):
                    src, nco = stream[sidx["i"]]
                    sidx["i"] += 1
                    return wload(src, nco)
                return None

            slots = []

            def need(j):
                while len(slots) <= j and sidx["i"] < len(stream):
                    slots.append(wnext())

            need(0)
            need(1)

            def gate_unit(b, cc, tg, lo, hi):
                c = b * 4 + cc
                bk = next_bank(lo, hi)
                proj_fm(slots[b], cc * 128, tg, bk)
                k.op(act, lambda: A.activation(out=y0(c, tg * 512, (tg + 1) * 512), in_=banks[bk],
                                               func=AF.Silu), reads=[bankb[bk]], writes=[R1bf[c][tg]])

            phase_norm_in(x_in, None, 0, hook=lambda tg: [gate_unit(0, cc, tg, 4, 8) for cc in range(4)])
            for b in range(1, 4):
                need(b + 1)
                for cc in range(4):
                    for tg in range(4):
                        gate_unit(b, cc, tg, 0, 4)
            with ExitStack() as a:
                qT = [a.enter_context(nc.sbuf_tensor(un("qT"), [128, 2048], BF16)) for _ in range(2)]
                kT = [a.enter_context(nc.sbuf_tensor(un("kT"), [128, 2048], BF16)) for _ in range(2)]
                Vt = [a.enter_context(nc.sbuf_tensor(un("Vt"), [128, 16, 128], BF16)) for _ in range(2)]
                Et = [a.enter_context(nc.sbuf_tensor(un("Et%d" % i), [128, 512], F32)) for i in range(2)]
                Etb = [Buf(), Buf()]
                Lt = [a.enter_context(nc.sbuf_tensor(un("Lt%d" % i), [128, 512], F32)) for i in range(3)]
                At = [a.enter_context(nc.sbuf_tensor(un("At%d" % i), [128, 512], BF16)) for i in range(3)]
                LS = [a.enter_context(nc.sbuf_tensor(un("LS%d" % i), [128, 512], F32)) for i in range(2)]
                qb = [[Buf() for _ in range(4)] for _ in range(2)]
                kb = [[Buf() for _ in range(4)] for _ in range(2)]
                vb = [[Buf() for _ in range(4)] for _ in range(2)]
                Ltb = [Buf() for _ in range(3)]
                Atb = [Buf() for _ in range(3)]
                LSb = [Buf(), Buf()]
                gstep = {"n": 0}
                pjb = {"n": 0}
                PJ = (0, 1, 7)

                def pj_bank():
                    b_ = PJ[pjb["n"] % 3]
                    pjb["n"] += 1
                    return b_

                def pe_raw(fns, reads, writes):
                    k._deps(pe, reads, writes)
                    for fn in fns:
                        fn()

                def proj_units(h):
                    s = slots[4 + h]
                    P_ = h % 2
                    units = []

                    def qk_piece(col0, tg, part, st_):
                        if part == 0:
                            st_["bk"] = pj_bank()
                        bk = st_["bk"]
                        fns = [lambda dc=dc: T.matmul(banks[bk], lhsT=wsl[s][:, dc, col0:col0 + 128],
                                                      rhs=R0[:, dc, tg * 512:(tg + 1) * 512],
                                                      start=(dc == 0), stop=(dc == 15))
                               for dc in range(part * 4, part * 4 + 4)]
                        rds = [wslb[s]] + r0_tg(tg)
                        if part < 3:
                            pe_raw(fns, rds, [bankb[bk]])
                            return
                        k.op(pe, fns, reads=rds, writes=[bankb[bk]])
                        if col0 == 0:
                            k.op(dve, lambda: V.tensor_scalar(out=qT[P_][:, tg * 512:(tg + 1) * 512], in0=banks[bk],
                                                              scalar1=QSCALE, scalar2=None, op0=ALU.mult),
                                 reads=[bankb[bk]], writes=[qb[P_][tg]])
                        else:
                            k.op(dve, lambda: V.tensor_copy(out=kT[P_][:, tg * 512:(tg + 1) * 512], in_=banks[bk]),
                                 reads=[bankb[bk]], writes=[kb[P_][tg]])

                    def v_piece(tbg, kk, st_):
                        if kk == 0:
                            st_["bk"] = pj_bank()
                        bk = st_["bk"]
                        tb = tbg * 4 + kk
                        fns = [lambda dc=dc: T.matmul(banks[bk][:, kk * 128:(kk + 1) * 128],
                                                      lhsT=R0[:, dc, tb * 128:(tb + 1) * 128],
                                                      rhs=wsl[s][:, dc, 256:384], start=(dc == 0), stop=(dc == 15))
                               for dc in range(16)]
                        rds = [wslb[s]] + r0_tg(tbg)
                        if kk < 3:
                            pe_raw(fns, rds, [bankb[bk]])
                            return
                        k.op(pe, fns, reads=rds, writes=[bankb[bk]])
                        k.op(dve, lambda: V.tensor_copy(out=Vt[P_][:, tbg * 4:(tbg + 1) * 4, :],
                                                        in_=banks[bk].rearrange("p (a b) -> p a b", a=4)),
                             reads=[bankb[bk]], writes=[vb[P_][tbg]])

                    for tg in range(4):
                        for col0 in (0, 128):
                            st_ = {}
                            for part in range(4):
                                units.append(lambda col0=col0, tg=tg, part=part, st_=st_: qk_piece(col0, tg, part, st_))
                    for tbg in range(4):
                        st_ = {}
                        for kk in range(4):
                            units.append(lambda tbg=tbg, kk=kk, st_=st_: v_piece(tbg, kk, st_))
                    return units

                need(4)
                need(5)
                for u in proj_units(0):
                    u()
                for h in range(8):
                    need(4 + h + 2)
                    P = h % 2
                    units = proj_units(h + 1) if h + 1 < 8 else []
                    nun = len(units)
                    steps = []
                    for Gq in range(4):
                        for sbk in range(4 * Gq + 3, -1, -1):
                            diag = sbk >= 4 * Gq
                            c0 = (sbk - 4 * Gq) * 128 if diag else 0
                            steps.append(dict(G=Gq, sb=sbk, diag=diag, c0=c0, n=512 - c0,
                                              first=(sbk == 4 * Gq + 3), last=(sbk == 0), id=gstep["n"]))
                            gstep["n"] += 1
                    lsc = {"cur": 0}

                    def s1(sp_):
                        i_ = sp_["id"]
                        n, c0, Gq, sbk = sp_["n"], sp_["c0"], sp_["G"], sp_["sb"]
                        bz = 2 + i_ % 3
                        q0 = Gq * 512 + c0
                        k.op(pe, lambda: T.matmul(banks[bz][:, 0:n], lhsT=kT[P][:, sbk * 128:(sbk + 1) * 128],
                                                  rhs=qT[P][:, q0:q0 + n], start=True, stop=True),
                             reads=[kb[P][sbk // 4], qb[P][Gq]], writes=[bankb[bz]])
                        l, lb = Lt[i_ % 3], Ltb[i_ % 3]
                        e, eb = Et[i_ % 2], Etb[i_ % 2]
                        k.op(act, lambda: A.activation(out=e[:, 0:n], in_=banks[bz][:, 0:n], func=AF.Exp),
                             reads=[bankb[bz]], writes=[eb])
                        k.op(act, lambda: A.activation(out=l[:, 0:n].bitcast(F32R), in_=e[:, 0:n], func=AF.Ln,
                                                       bias=1.0, scale=1.0), reads=[eb], writes=[lb])
                        if sp_["diag"]:
                            k.op(dve, lambda: V.tensor_tensor(out=l[:, 0:128].bitcast(F32R), in0=l[:, 0:128],
                                                              in1=maskS, op=ALU.mult), reads=[lb, cstb], writes=[lb])

                    def s2(sp_):
                        i_ = sp_["id"]
                        n, c0, Gq, sbk = sp_["n"], sp_["c0"], sp_["G"], sp_["sb"]
                        bz = 2 + i_ % 3
                        l, lb = Lt[i_ % 3], Ltb[i_ % 3]
                        if sp_["first"]:
                            for j in range(2):
                                k.op(pool, lambda j=j: G.tensor_tensor(out=LS[j][:].bitcast(F32R), in0=cst[:, 0:512],
                                                                       in1=cst[:, 0:512], op=ALU.subtract),
                                     reads=[cstb], writes=[LSb[j]])
                            lsc["cur"] = 0
                        cur = lsc["cur"]
                        fns = [lambda: T.matmul(banks[bz][:, 0:n], lhsT=negTRI, rhs=l[:, 0:n].bitcast(F32R),
                                                start=False, stop=sp_["first"], skip_group_check=True)]
                        rds = [lb, crb]
                        if not sp_["first"]:
                            fns.append(lambda: T.matmul(banks[bz][:, 0:n], lhsT=negONES,
                                                        rhs=LS[cur][:, c0:512].bitcast(F32R), start=False, stop=True,
                                                        skip_group_check=True))
                            rds.append(LSb[cur])
                        k.op(pe, fns, reads=rds, writes=[bankb[bz]])
                        if not sp_["last"]:
                            k.op(pool, lambda: G.tensor_tensor(out=LS[1 - cur][:, c0:512].bitcast(F32R),
                                                               in0=LS[cur][:, c0:512], in1=l[:, 0:n], op=ALU.add),
                                 reads=[LSb[cur], lb], writes=[LSb[1 - cur]])
                            lsc["cur"] = 1 - cur
                        at, ab = At[i_ % 3], Atb[i_ % 3]
                        k.op(act, lambda: A.activation(out=at[:, 0:n], in_=banks[bz][:, 0:n], func=AF.Exp),
                             reads=[bankb[bz]], writes=[ab])
                        if sp_["diag"]:
                            k.op(dve, lambda: V.tensor_tensor(out=at[:, 0:128], in0=at[:, 0:128], in1=maskS,
                                                              op=ALU.mult), reads=[ab, cstb], writes=[ab])

                    def s3(sp_):
                        i_ = sp_["id"]
                        n, c0, Gq, sbk = sp_["n"], sp_["c0"], sp_["G"], sp_["sb"]
                        bo = 5 + Gq % 2
                        at, ab = At[i_ % 3], Atb[i_ % 3]
                        fns = []
                        if sp_["first"]:
                            fns.append(lambda: T.matmul(banks[bo], lhsT=zb[:], rhs=qT[P][:, Gq * 512:(Gq + 1) * 512],
                                                        start=True, stop=False))
                        fns.append(lambda: T.matmul(banks[bo][:, c0:512], lhsT=Vt[P][:, sbk, :], rhs=at[:, 0:n],
                                                    start=False, stop=sp_["last"]))
                        k.op(pe, fns, reads=[vb[P][sbk // 4], ab, zbb, qb[P][Gq]], writes=[bankb[bo]])
                        if sp_["last"]:
                            yo = y0(h, Gq * 512, (Gq + 1) * 512)
                            k.op(dve, lambda: V.tensor_tensor(out=yo, in0=banks[bo], in1=yo, op=ALU.mult),
                                 reads=[bankb[bo], R1bf[h][Gq]], writes=[R1bf[h][Gq]])

                    ns = len(steps)
                    for t in range(ns + 2):
                        if t < ns:
                            s1(steps[t])
                        if 0 <= t - 1 < ns:
                            s2(steps[t - 1])
                        if 0 <= t - 2 < ns:
                            s3(steps[t - 2])
                        tgt = (t + 1) * nun // (ns + 2)
                        while units and nun - len(units) < tgt:
                            units.pop(0)()
                    while units:
                        units.pop(0)()
                k.fence()
            with ExitStack() as a:
                U = a.enter_context(nc.sbuf_tensor(un("U"), [128, 2064], F32))
                PA = a.enter_context(nc.sbuf_tensor(un("PA"), [128, 2064], F32))
                PB = a.enter_context(nc.sbuf_tensor(un("PB"), [128, 2064], F32))
                Pt = [a.enter_context(nc.sbuf_tensor(un("Pt%d" % i), [128, 2048], BF16)) for i in range(2)]
                tf = a.enter_context(nc.sbuf_tensor(un("tf"), [128, 16], F32))
                pw = a.enter_context(nc.sbuf_tensor(un("pw_s"), [128, 4 * 2 * 256], BF16))
                k.dma(pool, pw[:], pw_d, owner=pwb, writes=[pwb])
                Ub, PAb, PBb, tfb = Buf(), Buf(), Buf(), Buf()
                Ptb = [Buf(), Buf()]
                for t_, b_ in ((U, Ub), (PA, PAb), (PB, PBb)):
                    k.op(dve, lambda t_=t_: V.memset(t_[:, 0:16], 0.0), writes=[b_])

                def pool_proj(gi, cc):
                    s = slots[12 + gi]
                    bks = []
                    for tg in range(4):
                        bk = next_bank(0, 8)
                        proj_fm(s, cc * 128, tg, bk)
                        bks.append(bk)
                        if gi == 3 and cc == 1:
                            prefetch_wo(w0o, tg)
                    return bks

                def pool_evac(bks):
                    for tg in range(4):
                        bk = bks[tg]
                        o = U[:, 16 + tg * 512:16 + (tg + 1) * 512]
                        k.op(act, lambda: A.activation(out=o, in_=banks[bk], func=AF.Copy),
                             reads=[bankb[bk]], writes=[Ub])

                def pool_chain(gi, cc):
                    w = POOL_WINDOWS[gi]
                    m = gi + 1
                    cur, curb = U, Ub
                    for lvl in range(m):
                        sh = 1 << lvl
                        dst, dstb = (PA, PAb) if lvl % 2 == 0 else (PB, PBb)
                        k.op(dve, lambda: V.tensor_tensor(out=dst[:, 16:2064], in0=cur[:, 16:2064],
                                                          in1=cur[:, 16 - sh:2064 - sh], op=ALU.add),
                             reads=[curb], writes=[dstb])
                        cur, curb = dst, dstb
                    k.op(dve, lambda: V.scalar_tensor_tensor(out=Pt[cc][:], in0=cur[:, 16:2064], scalar=1.0 / w,
                                                             in1=U[:, 16:2064], op0=ALU.mult, op1=ALU.subtract),
                         reads=[curb, Ub], writes=[Ptb[cc]])
                    k.op(dve, lambda: V.tensor_tensor(out=tf[:, 0:w - 1], in0=cur[:, 16:16 + w - 1],
                                                      in1=invc[:, gi * 16:gi * 16 + w - 1], op=ALU.mult),
                         reads=[curb, cstb], writes=[tfb])
                    k.op(dve, lambda: V.tensor_tensor(out=Pt[cc][:, 0:w - 1], in0=tf[:, 0:w - 1],
                                                      in1=U[:, 16:16 + w - 1], op=ALU.subtract),
                         reads=[tfb, Ub], writes=[Ptb[cc]])

                def pool_linear(gi):
                    for oc in range(2):
                        ch = 2 * gi + oc
                        for tg in range(4):
                            bk = next_bank(0, 8)
                            fns = [lambda kc=kc: T.matmul(
                                banks[bk], lhsT=pw[:, (gi * 2 + kc) * 256 + oc * 128:(gi * 2 + kc) * 256 + (oc + 1) * 128],
                                rhs=Pt[kc][:, tg * 512:(tg + 1) * 512], start=(kc == 0), stop=(kc == 1))
                                for kc in range(2)]
                            k.op(pe, fns, reads=[pwb, Ptb[0], Ptb[1]], writes=[bankb[bk]])
                            yo = y0(8 + ch, tg * 512, (tg + 1) * 512)
                            k.op(dve, lambda: V.scalar_tensor_tensor(out=yo, in0=banks[bk], scalar=psc[:, ch:ch + 1],
                                                                     in1=yo, op0=ALU.mult, op1=ALU.mult),
                                 reads=[bankb[bk], fmb, R1bf[8 + ch][tg]], writes=[R1bf[8 + ch][tg]])

                pend = None
                for gi in range(4):
                    need(12 + gi + 1)
                    b0 = pool_proj(gi, 0)
                    pool_evac(b0)
                    if pend is not None:
                        pool_linear(pend)
                    pool_chain(gi, 0)
                    b1 = pool_proj(gi, 1)
                    pool_evac(b1)
                    pool_chain(gi, 1)
                    pend = gi
                pool_linear(pend)
                k.fence()
            return (lambda fc, lo, hi: y0(fc, lo, hi)), (lambda tg: [R1bf[c][tg] for c in range(16)])

        def layer1():
            R1fb = [[Buf() for _ in range(4)] for _ in range(8)]

            def ybf(c16, lo, hi):
                return R1[:, c16 // 2, :].bitcast(BF16)[:, (c16 % 2) * 2048 + lo:(c16 % 2) * 2048 + hi]

            def ybuf(c16, tg):
                return R1fb[c16 // 2][(c16 % 2) * 2 + tg // 2]

            stream = [(w1c[c], 256) for c in range(8)] + [(w1g[c], 128) for c in range(8)] + \
                     [(w1s[c], 512) for c in range(8)]
            sidx = {"i": 0}

            def wnext():
                if sidx["i"] < len(stream):
                    src, nco = stream[sidx["i"]]
                    sidx["i"] += 1
                    return wload(src, nco)
                return None

            slots = [wnext()]
            with ExitStack() as a:
                dpad = [a.enter_context(nc.sbuf_tensor(un("dpad%d" % i), [128, 2080], F32)) for i in range(2)]
                sig = [a.enter_context(nc.sbuf_tensor(un("sig%d" % i), [128, 512], F32)) for i in range(4)]
                dpb = [Buf(), Buf()]
                sigb = [Buf() for _ in range(4)]
                NPT = 16
                dpr = a.enter_context(nc.sbuf_tensor(un("dpr"), [128, 2080], F32))
                dg = a.enter_context(nc.sbuf_tensor(un("dg"), [128, NPT * 128], F32))
                dprb, dgb = Buf(), Buf()
                cvb = {"n": 0}
                for j in range(2):
                    k.op(dve, lambda j=j: V.memset(dpad[j][:, 0:32], 0.0), writes=[dpb[j]])
                k.op(dve, lambda: V.tensor_copy(out=dpr[:, 0:32].bitcast(F32R), in_=dpad[0][:, 0:32]),
                     reads=[dpb[0]], writes=[dprb])
                def stage_a_pe(c, cb=None):
                    slots.append(wnext())
                    s = slots[c]
                    for tg in range(4):
                        b1 = 4 + tg % 2
                        proj_fm(s, 128, tg, b1)
                        k.op(act, lambda: A.activation(out=sig[tg][:], in_=banks[b1], func=AF.Sigmoid),
                             reads=[bankb[b1]], writes=[sigb[tg]])
                        proj_fm(s, 0, tg, tg)
                        if cb is not None:
                            cb(tg)

                def stage_a_dve(c):
                    dp, dpbb = dpad[c % 2], dpb[c % 2]
                    for tg in range(4):
                        k.op(dve, lambda: V.tensor_tensor(out=dp[:, 32 + tg * 512:32 + (tg + 1) * 512], in0=banks[tg],
                                                          in1=sig[tg][:], op=ALU.mult),
                             reads=[bankb[tg], sigb[tg]], writes=[dpbb])

                def stage_b_act(c):
                    dp, dpbb = dpad[c % 2], dpb[c % 2]
                    for tg in range(4):
                        k.op(act, lambda: A.activation(out=R1[:, c, tg * 512:(tg + 1) * 512],
                                                       in_=dp[:, 32 + tg * 512:32 + (tg + 1) * 512], func=AF.Identity,
                                                       scale=dwv(c, 30), bias=dbv(c)),
                             reads=[dpbb, fmb], writes=[R1fb[c][tg]])
                    for tg in range(4):
                        sl32 = slice(32 + tg * 512, 32 + (tg + 1) * 512)
                        k.op(act, lambda: A.activation(out=dpr[:, sl32].bitcast(F32R), in_=dp[:, sl32], func=AF.Copy),
                             reads=[dpbb], writes=[dprb])

                def stage_b_diag(c, part):
                    for kk in range(part * NPT // 4, (part + 1) * NPT // 4):
                        k.op(act, lambda kk=kk: A.activation(out=dg[:, kk * 128:(kk + 1) * 128].bitcast(F32R),
                                                             in_=ident, func=AF.Copy, scale=dwv(c, kk)),
                             reads=[cstb, fmb], writes=[dgb])

                def stage_b_dve(c):
                    dp, dpbb = dpad[c % 2], dpb[c % 2]
                    dst = R1[:, c, :]
                    for kk in range(NPT, 30):
                        k.op(dve, lambda kk=kk: V.scalar_tensor_tensor(out=dst, in0=dp[:, 2 + kk:2 + kk + 2048],
                                                                       scalar=dwv(c, kk), in1=dst,
                                                                       op0=ALU.mult, op1=ALU.add),
                             reads=[dpbb, fmb], writes=R1fb[c])

                def stage_b_pe(c):
                    dst = R1[:, c, :]
                    for tg in range(4):
                        sl = slice(tg * 512, (tg + 1) * 512)
                        bk = 6 + cvb["n"] % 2
                        cvb["n"] += 1
                        fns = [lambda kk=kk: T.matmul(banks[bk], lhsT=dg[:, kk * 128:(kk + 1) * 128].bitcast(F32R),
                                                      rhs=dpr[:, 2 + kk + tg * 512:2 + kk + (tg + 1) * 512].bitcast(F32R),
                                                      start=(kk == 0), stop=(kk == NPT - 1)) for kk in range(NPT)]
                        k.op(pe, fns, reads=[dgb, dprb], writes=[bankb[bk]])
                        k.op(dve, lambda: V.tensor_tensor(out=dst[:, sl], in0=banks[bk], in1=dst[:, sl], op=ALU.add),
                             reads=[bankb[bk]] + R1fb[c], writes=R1fb[c])

                stage_a_pe(0)
                stage_a_dve(0)
                for c in range(8):
                    stage_b_act(c)
                    if c + 1 < 8:
                        stage_a_pe(c + 1, cb=lambda tg, c=c: stage_b_diag(c, tg))
                    else:
                        for part in range(4):
                            stage_b_diag(c, part)
                    stage_b_dve(c)
                    stage_b_pe(c)
                    if c + 1 < 8:
                        stage_a_dve(c + 1)
                k.fence()
            with ExitStack() as a:
                mean = a.enter_context(nc.sbuf_tensor(un("mean"), [128, 2048], F32))
                rstd = a.enter_context(nc.sbuf_tensor(un("rstd"), [128, 2048], F32))
                sq = [a.enter_context(nc.sbuf_tensor(un("sq%d" % i), [128, 512], F32)) for i in range(2)]
                tmc = [a.enter_context(nc.sbuf_tensor(un("tmc%d" % i), [128, 512], F32)) for i in range(2)]
                sgt = [a.enter_context(nc.sbuf_tensor(un("sgt%d" % i), [128, 512], F32)) for i in range(2)]
                meanb = [Buf() for _ in range(4)]
                rstdb = [Buf() for _ in range(4)]
                tmcb, sgtb = [Buf(), Buf()], [Buf(), Buf()]
                cp = [a.enter_context(nc.sbuf_tensor(un("cp%d" % i), [128, 512], F32)) for i in range(3)]
                sq3 = a.enter_context(nc.sbuf_tensor(un("sq2"), [128, 512], F32))
                sqs = [sq[0], sq[1], sq3]
                cpb = [Buf() for _ in range(3)]
                sqb = [Buf() for _ in range(3)]
                n_ = 0
                for tg in range(4):
                    for c in range(8):
                        r_ = n_ % 3
                        n_ += 1
                        src = R1[:, c, tg * 512:(tg + 1) * 512]
                        k.op(dve, lambda: V.tensor_copy(out=cp[r_][:].bitcast(F32R), in_=src),
                             reads=[R1fb[c][tg]], writes=[cpb[r_]])
                        k.op(pe, lambda: T.matmul(banks[tg], lhsT=negONES, rhs=cp[r_][:].bitcast(F32R),
                                                  start=(c == 0), stop=(c == 7)),
                             reads=[crb, cpb[r_]], writes=[bankb[tg]])
                        k.op(act, lambda: A.activation(out=sqs[r_][:].bitcast(F32R), in_=src, func=AF.Square),
                             reads=[R1fb[c][tg]], writes=[sqb[r_]])
                        k.op(pe, lambda: T.matmul(banks[4 + tg], lhsT=negONES, rhs=sqs[r_][:].bitcast(F32R),
                                                  start=(c == 0), stop=(c == 7)),
                             reads=[crb, sqb[r_]], writes=[bankb[4 + tg]])
                    sl = slice(tg * 512, (tg + 1) * 512)
                    k.op(act, lambda: A.activation(out=mean[:, sl], in_=banks[tg], func=AF.Copy, scale=-1.0 / 1024),
                         reads=[bankb[tg]], writes=[meanb[tg]])
                    k.op(dve, lambda: V.tensor_tensor(out=rstd[:, sl], in0=mean[:, sl], in1=mean[:, sl], op=ALU.mult),
                         reads=[meanb[tg]], writes=[rstdb[tg]])
                    k.op(dve, lambda: V.scalar_tensor_tensor(out=rstd[:, sl], in0=banks[4 + tg], scalar=-1.0 / 1024,
                                                             in1=rstd[:, sl], op0=ALU.mult, op1=ALU.subtract),
                         reads=[bankb[4 + tg], rstdb[tg]], writes=[rstdb[tg]])
                    k.op(dve, lambda: V.tensor_scalar(out=rstd[:, sl], in0=rstd[:, sl], scalar1=EPS, scalar2=None,
                                                      op0=ALU.add), reads=[rstdb[tg]], writes=[rstdb[tg]])
                    k.op(act, lambda: A.activation(out=rstd[:, sl], in_=rstd[:, sl], func=AF.Sqrt),
                         reads=[rstdb[tg]], writes=[rstdb[tg]])
                    k.op(dve, lambda: V.reciprocal(out=rstd[:, sl], in_=rstd[:, sl]), reads=[rstdb[tg]],
                         writes=[rstdb[tg]])
                n_ = 0
                for c in range(8):
                    slots.append(wnext())
                    s = slots[8 + c]
                    for tg in range(4):
                        sl = slice(tg * 512, (tg + 1) * 512)
                        bk = next_bank(0, 8)
                        proj_fm(s, 0, tg, bk)
                        g_, gb_ = sgt[n_ % 2], sgtb[n_ % 2]
                        t_, tb_ = tmc[n_ % 2], tmcb[n_ % 2]
                        n_ += 1
                        k.op(act, lambda: A.activation(out=g_[:], in_=banks[bk], func=AF.Silu),
                             reads=[bankb[bk]], writes=[gb_])
                        k.op(dve, lambda: V.tensor_tensor(out=t_[:], in0=R1[:, c, sl], in1=mean[:, sl], op=ALU.subtract),
                             reads=[R1fb[c][tg], meanb[tg]], writes=[tb_])
                        k.op(dve, lambda: V.tensor_tensor(out=t_[:], in0=t_[:], in1=rstd[:, sl], op=ALU.mult),
                             reads=[tb_, rstdb[tg]], writes=[tb_])
                        k.op(act, lambda: A.activation(out=t_[:], in_=t_[:], func=AF.Silu, scale=cgv(c), bias=cbv(c)),
                             reads=[tb_, fmb], writes=[tb_])
                        k.op(dve, lambda: V.tensor_tensor(out=ybf(2 * c, tg * 512, (tg + 1) * 512), in0=t_[:],
                                                          in1=g_[:], op=ALU.mult),
                             reads=[tb_, gb_, R1fb[c][tg]], writes=[ybuf(2 * c, tg)])
                k.fence()
            with ExitStack() as a:
                Tt = a.enter_context(nc.sbuf_tensor(un("Tt"), [128, 2052], F32))
                acc = a.enter_context(nc.sbuf_tensor(un("acc"), [128, 2048], F32))
                bcsg = a.enter_context(nc.sbuf_tensor(un("bcsg"), [128, 2048], F32))
                hct = [a.enter_context(nc.sbuf_tensor(un("hct%d" % i), [128, 512], F32)) for i in range(2)]
                sgt = [a.enter_context(nc.sbuf_tensor(un("sgd%d" % i), [128, 512], F32)) for i in range(2)]
                Ttb, accb, bcsgb = Buf(), Buf(), Buf()
                hctb, sgtb = [Buf(), Buf()], [Buf(), Buf()]
                k.op(dve, lambda: V.memset(Tt[:, 0:4], 0.0), writes=[Ttb])
                for c in range(8):
                    slots.append(wnext())
                    s = slots[16 + c]
                    for tg in range(4):
                        sl = slice(tg * 512, (tg + 1) * 512)
                        g_, gb_ = sgt[tg % 2], sgtb[tg % 2]
                        h_, hb_ = hct[tg % 2], hctb[tg % 2]
                        bk = next_bank(0, 8)
                        proj_fm(s, 384, tg, bk)
                        k.op(act, lambda: A.activation(out=g_[:], in_=banks[bk], func=AF.Silu),
                             reads=[bankb[bk]], writes=[gb_])
                        bk2 = next_bank(0, 8)
                        proj_fm(s, 128, tg, bk2)
                        k.op(dve, lambda: V.tensor_tensor(out=bcsg[:, sl], in0=banks[bk2], in1=g_[:], op=ALU.mult),
                             reads=[bankb[bk2], gb_], writes=[bcsgb])
                        bk3 = next_bank(0, 8)
                        proj_fm(s, 0, tg, bk3)
                        k.op(act, lambda: A.activation(out=h_[:], in_=banks[bk3], func=AF.Copy),
                             reads=[bankb[bk3]], writes=[hb_])
                        bk4 = next_bank(0, 8)
                        proj_fm(s, 256, tg, bk4)
                        if c == 7:
                            prefetch_wo(w1o, tg)
                        k.op(dve, lambda: V.tensor_tensor(out=Tt[:, 4 + tg * 512:4 + (tg + 1) * 512], in0=banks[bk4],
                                                          in1=h_[:], op=ALU.mult),
                             reads=[bankb[bk4], hb_], writes=[Ttb])
                    k.op(dve, lambda: V.tensor_scalar(out=acc[:], in0=Tt[:, 4:2052], scalar1=swv(c, 2), scalar2=None,
                                                      op0=ALU.mult), reads=[Ttb, fmb], writes=[accb])
                    k.op(dve, lambda: V.scalar_tensor_tensor(out=acc[:], in0=Tt[:, 3:2051], scalar=swv(c, 1),
                                                             in1=acc[:], op0=ALU.mult, op1=ALU.add),
                         reads=[Ttb, fmb], writes=[accb])
                    k.op(dve, lambda: V.scalar_tensor_tensor(out=acc[:], in0=Tt[:, 2:2050], scalar=swv(c, 0),
                                                             in1=acc[:], op0=ALU.mult, op1=ALU.add),
                         reads=[Ttb, fmb], writes=[accb])
                    for tg in range(4):
                        sl = slice(tg * 512, (tg + 1) * 512)
                        k.op(dve, lambda: V.tensor_tensor(out=ybf(2 * c + 1, tg * 512, (tg + 1) * 512), in0=acc[:, sl],
                                                          in1=bcsg[:, sl], op=ALU.mult),
                             reads=[accb, bcsgb], writes=[ybuf(2 * c + 1, tg)])
                k.fence()

            def ysrc(fc, lo, hi):
                c16 = 2 * fc + 1 if fc < 8 else 2 * (fc - 8)
                return ybf(c16, lo, hi)

            def ybufs(tg):
                return [R1fb[c][q] for c in range(8) for q in range(4)]

            return ysrc, ybufs

        k.fence()
        if layers == (0, 1):
            ysrc, ybufs = layer0()
            nss = [Buf() for _ in range(16)]
            phase_out(x_in, None, 1, w0o, ysrc, ybufs, x1_d, x1b, nxt=nss)
            phase_norm_in(x1_d, x1b, 2, pre=nss)
            ysrc, ybufs = layer1()
            phase_out(x1_d, x1b, 3, w1o, ysrc, ybufs, out_d, None)
        elif layers == (0,):
            ysrc, ybufs = layer0()
            phase_out(x_in, None, 1, w0o, ysrc, ybufs, out_d, None)
        else:
            phase_norm_in(x_in, None, 2)
            ysrc, ybufs = layer1()
            phase_out(x_in, None, 3, w1o, ysrc, ybufs, out_d, None)
        k.fence()
    return nc


def _tile(w):
    return np.ascontiguousarray(w.reshape(16, 128, w.shape[1]))


def prep_weights(ln_pre_even, w_in_even, pool_w, pool_scale, w_out_even, ln_post_even,
                 ln_pre_odd, w_in_odd, sconv_w, dconv_w, dconv_b, cnorm_g, cnorm_b, w_out_odd, ln_post_odd):
    f = np.float32
    W0 = np.asarray(w_in_even, f)[0]
    W1 = np.asarray(w_in_odd, f)[0]
    wo0 = np.asarray(w_out_even, f)[0]
    wo1 = np.asarray(w_out_odd, f)[0]
    m = {}
    m["w0g"] = np.stack([_tile(W0[:, 4096 + b * 512:4096 + (b + 1) * 512]) for b in range(4)])
    m["w0h"] = np.stack([_tile(np.concatenate([W0[:, h * 128:(h + 1) * 128], W0[:, 1024 + h * 128:1024 + (h + 1) * 128],
                                               W0[:, 2048 + h * 128:2048 + (h + 1) * 128]], axis=1)) for h in range(8)])
    m["w0u"] = np.stack([_tile(W0[:, 3072 + g * 256:3072 + (g + 1) * 256]) for g in range(4)])
    m["w0o"] = np.stack([_tile(wo0[:, b * 512:(b + 1) * 512]) for b in range(4)])
    pw = np.asarray(pool_w, f)[0]
    m["pw"] = np.ascontiguousarray(pw.reshape(4, 2, 128, 256).transpose(2, 0, 1, 3).reshape(128, 2048))
    m["w1c"] = np.stack([_tile(np.concatenate([W1[:, 3072 + c * 128:3072 + (c + 1) * 128],
                                               W1[:, 4096 + c * 128:4096 + (c + 1) * 128]], axis=1)) for c in range(8)])
    m["w1g"] = np.stack([_tile(W1[:, 5120 + (8 + c) * 128:5120 + (9 + c) * 128]) for c in range(8)])
    m["w1s"] = np.stack([_tile(np.concatenate([W1[:, c * 128:(c + 1) * 128], W1[:, 1024 + c * 128:1024 + (c + 1) * 128],
                                               W1[:, 2048 + c * 128:2048 + (c + 1) * 128],
                                               W1[:, 5120 + c * 128:5120 + (c + 1) * 128]], axis=1)) for c in range(8)])
    m["w1o"] = np.stack([_tile(wo1[:, b * 512:(b + 1) * 512]) for b in range(4)])
    gvs = [ln_pre_even, ln_post_even, ln_pre_odd, ln_post_odd]
    m["gb"] = np.stack([np.ascontiguousarray(np.broadcast_to(np.asarray(g, f)[0][None, :], (128, 2048))) for g in gvs])
    fm = np.zeros((128, 336), f)
    fm[:, 304:320] = np.asarray(ln_pre_even, f)[0].reshape(16, 128).T
    fm[:, 320:336] = np.asarray(ln_pre_odd, f)[0].reshape(16, 128).T
    fm[:, 0:8] = np.asarray(pool_scale, f)[0].reshape(8, 128).T
    fm[:, 8:32] = np.asarray(sconv_w, f)[0].reshape(3, 8, 128).transpose(2, 1, 0).reshape(128, 24)
    fm[:, 32:280] = np.asarray(dconv_w, f)[0].reshape(31, 8, 128).transpose(2, 1, 0).reshape(128, 248)
    fm[:, 280:288] = np.asarray(dconv_b, f)[0].reshape(8, 128).T
    fm[:, 288:296] = np.asarray(cnorm_g, f)[0].reshape(8, 128).T
    fm[:, 296:304] = np.asarray(cnorm_b, f)[0].reshape(8, 128).T
    m["fm"] = fm
    cst = np.zeros((128, 704), f)
    p = np.arange(128)
    cst[:, 0:128] = np.eye(128, dtype=f)
    cst[:, 128:256] = -(p[:, None] >= p[None, :]).astype(f)
    cst[:, 256:384] = -1.0
    cst[:, 384:512] = (p[:, None] < p[None, :]).astype(f)
    cst[:, 512:640] = 1.0
    for gi, w in enumerate(POOL_WINDOWS):
        t = np.arange(16)
        cst[:, 640 + gi * 16:640 + (gi + 1) * 16] = (1.0 / np.minimum(w, t + 1)).astype(f)[None, :]
    m["cst"] = cst
    return m


def kernel(x, ln_pre_even, w_in_even, pool_w, pool_scale, w_out_even, ln_post_even,
           ln_pre_odd, w_in_odd, sconv_w, dconv_w, dconv_b, cnorm_g, cnorm_b, w_out_odd, ln_post_odd):
    x = np.asarray(x, np.float32)
    m = prep_weights(ln_pre_even, w_in_even, pool_w, pool_scale, w_out_even, ln_post_even,
                     ln_pre_odd, w_in_odd, sconv_w, dconv_w, dconv_b, cnorm_g, cnorm_b, w_out_odd, ln_post_odd)
    nc = build((0, 1))
    in_maps = []
    for b in range(8):
        d = dict(m)
        d["x"] = np.ascontiguousarray(x[b])
        in_maps.append(d)
    res = run_bass_kernel_spmd(nc, in_maps, core_ids=list(range(8)))
    return np.stack([np.asarray(r["out"], np.float32) for r in res.results], axis=0)
```
